# Optimizing a Trainium2 kernel written in Bass

```python
import jax, jax.numpy as jnp
from jax import lax
import numpy as np

D_MODEL = 1024
BATCH = 8
SEQ = 2048
DEPTH = 2

CHUNK = 128
PLE_DIM = 256
A_HEADS = 8
A_HEAD_DIM = 64
A_WIDTH = A_HEADS * A_HEAD_DIM
B_GROUPS = 4
B_GROUP_DIM = 128
B_WIDTH = B_GROUPS * B_GROUP_DIM
C_GROUPS = 8
C_GROUP_DIM = 64
C_WIDTH = C_GROUPS * C_GROUP_DIM
CONV_WIDTH = 31
N_BRANCHES = 3
LN_EPS = 1e-5
DEEPNORM_ALPHA = (2 * DEPTH) ** 0.25
DEEPNORM_BETA = (8 * DEPTH) ** -0.25
IN_SIZES = (A_WIDTH, A_WIDTH, A_WIDTH, B_WIDTH, B_WIDTH, C_WIDTH, C_WIDTH, C_WIDTH,
            N_BRANCHES * D_MODEL, D_MODEL)
IN_WIDTH = sum(IN_SIZES)

kernel_name = "hybrid_gmlp_fnet_conformer_deepnorm"


def _standardize(x):
    xf = x.astype(jnp.float32)
    mu = jnp.mean(xf, axis=-1, keepdims=True)
    var = jnp.mean(jnp.square(xf - mu), axis=-1, keepdims=True)
    return ((xf - mu) * lax.rsqrt(var + LN_EPS)).astype(x.dtype)


def _layer_norm(x, g, b):
    return _standardize(x) * g + b


def _spatial_gating(u, v, ln_g, ln_b, w_s, b_s):
    u = jax.nn.gelu(u)
    v = _layer_norm(jax.nn.gelu(v), ln_g, ln_b)
    bsz, seq, _ = v.shape
    n_chunks = seq // CHUNK
    vh = v.reshape(bsz, n_chunks, CHUNK, A_HEADS, A_HEAD_DIM)
    mixed = jnp.einsum('hqk,bnkhd->bnqhd', w_s, vh) + jnp.transpose(b_s)[:, :, None]
    return u * mixed.reshape(bsz, seq, A_WIDTH)


def _fourier_mix(z):
    bsz, seq, _ = z.shape
    zg = z.astype(jnp.float32).reshape(bsz, seq, B_GROUPS, B_GROUP_DIM)
    f = jnp.fft.fftn(zg, axes=(1, 3), norm='ortho')
    return jnp.real(f).reshape(bsz, seq, B_WIDTH).astype(z.dtype)


def _conv_module(val, glu_gate, conv_w, conv_b, ln_g, ln_b):
    h = val * jax.nn.sigmoid(glu_gate)
    h = lax.conv_general_dilated(
        h, conv_w[:, None, :].astype(h.dtype), window_strides=(1,), padding='SAME',
        dimension_numbers=('NWC', 'WIO', 'NWC'),
        feature_group_count=C_WIDTH) + conv_b
    bsz, seq, _ = h.shape
    hg = _standardize(h.reshape(bsz, seq, C_GROUPS, C_GROUP_DIM))
    h = hg.reshape(bsz, seq, C_WIDTH) * ln_g + ln_b
    return jax.nn.silu(h)


def _layer(x, p_i, w_in, b_in, a_ln_g, a_ln_b, a_ws, a_bs, c_conv_w, c_conv_b,
           c_ln_g, c_ln_b, w_pa, w_pb, w_pc, w_out, b_out, w_ple, ln_g, ln_b):
    bsz, seq, _ = x.shape
    proj = x @ w_in + b_in
    split_points = [int(s) for s in np.cumsum(IN_SIZES)[:-1]]
    (a_u, a_v, a_g, b_z, b_g, c_val, c_glu, c_g, merge, ple_g) = jnp.split(
        proj, split_points, axis=-1)
    y_a = _spatial_gating(a_u, a_v, a_ln_g, a_ln_b, a_ws, a_bs) * jax.nn.silu(a_g)
    y_b = _fourier_mix(b_z) * jax.nn.silu(b_g)
    y_c = _conv_module(c_val, c_glu, c_conv_w, c_conv_b, c_ln_g, c_ln_b) * jax.nn.silu(c_g)
    gates = jax.nn.sigmoid(merge).reshape(bsz, seq, N_BRANCHES, D_MODEL)
    merged = (gates[:, :, 0] * (y_a @ w_pa)
              + gates[:, :, 1] * (y_b @ w_pb)
              + gates[:, :, 2] * (y_c @ w_pc))
    mix = merged @ w_out + b_out
    ple = jax.nn.sigmoid(ple_g) * (p_i @ w_ple)
    return _layer_norm(DEEPNORM_ALPHA * x + mix + ple, ln_g, ln_b)


def setup_inputs(seed: int = 0) -> dict:
    key = jax.random.key(seed)
    ks = jax.random.split(key, 24)
    f32 = jnp.float32

    def nrm(k, shape, scale):
        return jax.random.normal(k, shape, f32) * scale

    return {
        "x": nrm(ks[0], (BATCH, SEQ, D_MODEL), 1.0),
        "p": nrm(ks[1], (DEPTH, BATCH, SEQ, PLE_DIM), 1.0),
        "w_in": nrm(ks[2], (DEPTH, D_MODEL, IN_WIDTH), D_MODEL ** -0.5),
        "b_in": nrm(ks[3], (DEPTH, IN_WIDTH), 0.02),
        "a_ln_g": 1.0 + nrm(ks[4], (DEPTH, A_WIDTH), 0.02),
        "a_ln_b": nrm(ks[5], (DEPTH, A_WIDTH), 0.02),
        "a_ws": nrm(ks[6], (DEPTH, A_HEADS, CHUNK, CHUNK), CHUNK ** -0.5),
        "a_bs": 1.0 + nrm(ks[7], (DEPTH, A_HEADS, CHUNK), 0.1),
        "c_conv_w": nrm(ks[8], (DEPTH, CONV_WIDTH, C_WIDTH), CONV_WIDTH ** -0.5),
        "c_conv_b": nrm(ks[9], (DEPTH, C_WIDTH), 0.02),
        "c_ln_g": 1.0 + nrm(ks[10], (DEPTH, C_WIDTH), 0.02),
        "c_ln_b": nrm(ks[11], (DEPTH, C_WIDTH), 0.02),
        "w_pa": nrm(ks[12], (DEPTH, A_WIDTH, D_MODEL), DEEPNORM_BETA * A_WIDTH ** -0.5),
        "w_pb": nrm(ks[13], (DEPTH, B_WIDTH, D_MODEL), DEEPNORM_BETA * B_WIDTH ** -0.5),
        "w_pc": nrm(ks[14], (DEPTH, C_WIDTH, D_MODEL), DEEPNORM_BETA * C_WIDTH ** -0.5),
        "w_out": nrm(ks[15], (DEPTH, D_MODEL, D_MODEL), DEEPNORM_BETA * D_MODEL ** -0.5),
        "b_out": nrm(ks[16], (DEPTH, D_MODEL), 0.02),
        "w_ple": nrm(ks[17], (DEPTH, PLE_DIM, D_MODEL), DEEPNORM_BETA * PLE_DIM ** -0.5),
        "ln_g": 1.0 + nrm(ks[18], (DEPTH, D_MODEL), 0.02),
        "ln_b": nrm(ks[19], (DEPTH, D_MODEL), 0.02),
    }


def reference(x, p, w_in, b_in, a_ln_g, a_ln_b, a_ws, a_bs, c_conv_w, c_conv_b,
              c_ln_g, c_ln_b, w_pa, w_pb, w_pc, w_out, b_out, w_ple, ln_g, ln_b):
    for i in range(DEPTH):
        x = _layer(x, p[i], w_in[i], b_in[i], a_ln_g[i], a_ln_b[i], a_ws[i], a_bs[i],
                   c_conv_w[i], c_conv_b[i], c_ln_g[i], c_ln_b[i], w_pa[i], w_pb[i],
                   w_pc[i], w_out[i], b_out[i], w_ple[i], ln_g[i], ln_b[i])
    return x
```

```python
import numpy as np
import ml_dtypes
from contextlib import ExitStack

import concourse.bass as bass
import concourse.mybir as mybir
from concourse.bass_utils import run_bass_kernel_spmd

F32 = mybir.dt.float32
BF16 = mybir.dt.bfloat16
AF = mybir.ActivationFunctionType
ALU = mybir.AluOpType

T = 2048
DM = 1024
NCH = 8
NTB = 4
NTT = 16
DEPTH = 2
ALPHA = float((2 * DEPTH) ** 0.25)
EPS = 1e-5
RING = 7


class Op:
    __slots__ = ("eng", "fn", "reads", "writes", "dma_key", "waits", "marked",
                 "sig", "pos", "fence")

    def __init__(self, eng, fn, reads, writes, dma_key, pos):
        self.eng = eng
        self.fn = fn
        self.reads = reads
        self.writes = writes
        self.dma_key = dma_key
        self.waits = {}
        self.marked = False
        self.sig = None
        self.pos = pos
        self.fence = None

    def stream(self):
        return ("dma_" + self.dma_key) if self.dma_key is not None else ("eng_" + self.eng)


def _tags(r):
    t = r[0] if isinstance(r, tuple) else r
    if isinstance(t, str) and t.startswith("@"):
        return t[1:].split("+")
    return ()


class Sched:
    def __init__(self):
        self.ops = []
        self.n = 0

    def add(self, eng, fn, reads=(), writes=(), dma_key=None, pos=None):
        if pos is None:
            self.n += 1
            pos = float(self.n)
        op = Op(eng, fn, tuple(reads), tuple(writes), dma_key, pos)
        self.ops.append(op)
        return op

    def fence(self, tags):
        self.n += 1
        op = Op(None, None, (), (), None, float(self.n))
        op.fence = tuple(tags)
        self.ops.append(op)

    def cur(self):
        return float(self.n)

    def analyze(self):
        self.ops.sort(key=lambda o: o.pos)
        for i, op in enumerate(self.ops):
            op.pos = i
        last_w = {}
        readers = {}
        reg_acc = {}
        reg_fence = {}
        for op in self.ops:
            if op.fence is not None:
                for t in op.fence:
                    best = {}
                    for o in reg_acc.get(t, []) + reg_fence.get(t, []):
                        s = o.stream()
                        if s not in best or best[s].pos < o.pos:
                            best[s] = o
                    reg_fence[t] = list(best.values())
                    reg_acc[t] = []
                continue
            deps = {}
            def consider(d, raw):
                if d is op:
                    return
                d_async = d.dma_key is not None
                if (not d_async) and d.eng == op.eng and op.dma_key is None and not raw and op.eng != "pool":
                    return
                s = d.stream()
                cur = deps.get(s)
                if cur is None or cur.pos < d.pos:
                    deps[s] = d
            for r in op.reads:
                w = last_w.get(r)
                if w is not None:
                    consider(w, True)
            for r in op.writes:
                w = last_w.get(r)
                if w is not None:
                    consider(w, False)
                for rd in readers.get(r, ()):
                    consider(rd, False)
            for r in op.reads + op.writes:
                for t in _tags(r):
                    for d in reg_fence.get(t, ()):
                        consider(d, True)
                    lst = reg_acc.setdefault(t, [])
                    if not lst or lst[-1] is not op:
                        lst.append(op)
            for s, d in deps.items():
                d.marked = True
                op.waits[s] = d
            for r in op.reads:
                readers.setdefault(r, []).append(op)
            for r in op.writes:
                last_w[r] = op
                readers[r] = []
            for r in op.reads:
                lst = readers[r]
                if len(lst) > 12:
                    best = {}
                    for o in lst:
                        s = o.stream()
                        if s not in best or best[s].pos < o.pos:
                            best[s] = o
                    readers[r] = sorted(best.values(), key=lambda o: o.pos)
        cnt = {}
        for op in self.ops:
            if op.fence is not None:
                continue
            if op.dma_key is not None:
                key = op.stream()
                cnt[key] = cnt.get(key, 0) + 16
                op.sig = (key, cnt[key])
                op.marked = True
            elif op.marked:
                key = op.stream()
                cnt[key] = cnt.get(key, 0) + 1
                op.sig = (key, cnt[key])
        for op in self.ops:
            if op.dma_key is not None and op.dma_key.startswith("const"):
                op.sig = (op.sig[0], cnt[op.sig[0]])
        self.final = cnt

    def emit(self, nc, final_wait_keys=()):
        self.analyze()
        with ExitStack() as es:
            sems = {k: es.enter_context(nc.semaphore(k)) for k in sorted(self.final)}
            block = es.enter_context(nc.Block())

            def run(engname):
                def body(e):
                    waited = {}
                    for op in self.ops:
                        if op.eng != engname:
                            continue
                        for s, d in op.waits.items():
                            k, v = d.sig
                            if waited.get(k, 0) < v:
                                e.wait_ge(sems[k], v)
                                waited[k] = v
                        ins = op.fn(e)
                        if op.marked:
                            k, v = op.sig
                            ins.then_inc(sems[k], 16 if op.dma_key is not None else 1)
                    if engname == "sp":
                        for k in final_wait_keys:
                            kk = "dma_" + k
                            if kk in self.final:
                                e.wait_ge(sems[kk], self.final[kk])
                return body

            block.tensor(run("pe"))
            block.scalar(run("act"))
            block.vector(run("dve"))
            block.gpsimd(run("pool"))
            block.sync(run("sp"))


class Ring:
    def __init__(self, S, name, nslots, queue, tag=None):
        self.S = S
        self.name = name
        self.n = nslots
        self.queue = queue
        self.tag = tag
        self.start = S.cur()
        self.reqs = []

    def res(self, slot, k):
        if self.tag:
            return (self.tag, self.name, slot, k)
        return (self.name, slot, k)

    def request(self, dma_fns):
        j = len(self.reqs)
        slot = j % self.n
        self.reqs.append([dma_fns, None, None, self.S.cur()])
        return j, slot, [self.res(slot, k) for k in range(len(dma_fns))]

    def used(self, j):
        self.reqs[j][1] = self.S.cur()
        if self.reqs[j][2] is None:
            self.reqs[j][2] = self.S.cur()

    def finalize(self):
        for j, (fns, _, first, reqpos) in enumerate(self.reqs):
            slot = j % self.n
            if j < self.n:
                base = self.start + 1e-4 * (j + 1)
            else:
                lp = self.reqs[j - self.n][1]
                assert lp is not None, (self.name, j)
                base = lp + 0.3 + 1e-4 * j
            assert first is not None and base < first, (self.name, j, base, first)
            for k, fn in enumerate(fns):
                self.S.add(self.queue, (lambda fn, slot: lambda e: fn(e, slot))(fn, slot),
                           reads=[("xgate",)], writes=[self.res(slot, k)],
                           dma_key=f"{self.name}{slot}_{k}", pos=base + 1e-6 * k)


def build(NL=DEPTH):
    nc = bass.Bass("TRN2", target_bir_lowering=False)

    def dram(n, s, dt=F32, kind="ExternalInput"):
        return nc.dram_tensor(n, list(s), dt, kind=kind).ap()

    x_d = dram("x", [T, DM])
    p_d = dram("p", [DEPTH, T, 256])
    w_in_d = dram("w_in", [DEPTH, DM, 8192])
    binT_d = dram("binT", [128, DEPTH, 64])
    bav_d = dram("bav", [DEPTH, 1, 512])
    alg_d = dram("alg", [DEPTH, 1, 512])
    alb_d = dram("alb", [DEPTH, 1, 512])
    aws_d = dram("aws", [DEPTH, 8, 128, 128])
    abs_d = dram("absr", [DEPTH, 1, 1024])
    cw_d = dram("convw", [128, DEPTH, 4, 31])
    cvec_d = dram("cvec", [128, DEPTH, 3, 4])
    wpa_d = dram("w_pa", [DEPTH, 512, DM])
    wpb_d = dram("w_pb", [DEPTH, 512, DM])
    wpc_d = dram("w_pc", [DEPTH, 512, DM])
    wout_d = dram("w_out", [DEPTH, DM, DM])
    wple_d = dram("w_ple", [DEPTH, 256, DM])
    ovec_d = dram("ovec", [128, DEPTH, 3, 8])
    ident_d = dram("ident", [128, 128])
    chand_d = dram("chand", [128, 257], BF16)
    dft_d = dram("dft", [2, T, T], BF16)
    out_d = dram("out", [T, DM], F32, kind="ExternalOutput")

    S = Sched()
    es = ExitStack()
    with es:
        def sb(n, s, dt=F32):
            return es.enter_context(nc.sbuf_tensor(n, list(s), dt))

        xT32 = sb("xT32", [128, NCH, T])
        xTb = sb("xTb", [128, NCH, T], BF16)
        wring = sb("wring", [128, RING, 8, 128], BF16)
        ident = sb("ident_s", [128, 128])
        identb = sb("identb", [128, 128], BF16)
        onesb = sb("onesb", [128, 128], BF16)
        bdiag = sb("bdiag", [128, 128], BF16)
        chand = sb("chand_s", [128, 258], BF16)
        neghalf = sb("neghalf", [128, 2])
        binT = sb("binT_s", [128, DEPTH, 64])
        cvec = sb("cvec_s", [128, DEPTH, 3, 4])
        ovec = sb("ovec_s", [128, DEPTH, 3, 8])
        convw = sb("convw_s", [128, DEPTH, 4, 31])
        g8 = sb("g8", [128, DEPTH, 4])
        epsT = sb("epsT", [128, 1])
        AR_BYTES = 92 * 1024
        arena = sb("arena", [128, AR_BYTES // 4])
        psum = [es.enter_context(nc.psum_tensor(f"ps{i}", [128, 512], F32)) for i in range(8)]

        Y0, Y1, Y2, M_, TM = 0, 16384, 32768, 49152, 81920

        def av(off, shape, dt=F32):
            esz = 4 if dt == F32 else 2
            nel = int(np.prod(shape[1:]))
            nbytes = nel * esz
            assert off % 4 == 0 and nbytes % 4 == 0 and off + nbytes <= AR_BYTES, (off, shape)
            v = arena[:, off // 4:(off + nbytes) // 4]
            if dt != F32:
                v = v.bitcast(dt)
            if len(shape) == 3:
                v = v.rearrange("p (a b) -> p a b", a=shape[1])
            elif len(shape) == 4:
                v = v.rearrange("p (a b c) -> p a b c", a=shape[1], b=shape[2])
            return v

        class PS:
            reserved = set()
            i = 0

            @classmethod
            def next(cls):
                while True:
                    b = cls.i % 8
                    cls.i += 1
                    if b not in cls.reserved:
                        return b

        def psr(b):
            return [("ps", b), ("pp", b)]

        def mm(bank, out_ap, pairs, reads, start=True, stop=True, extra_w=()):
            pairs = list(pairs)

            def fn(e):
                ins = None
                n = len(pairs)
                for i, (l, r) in enumerate(pairs):
                    ins = e.matmul(out_ap, l, r, start=(start and i == 0), stop=(stop and i == n - 1))
                return ins
            return S.add("pe", fn, reads, psr(bank) + list(extra_w))

        def mm_multi(bank, groups, reads):
            groups = [(o, list(pr)) for o, pr in groups]

            def fn(e):
                ins = None
                for o, pr in groups:
                    n = len(pr)
                    for i, (l, r) in enumerate(pr):
                        ins = e.matmul(o, l, r, start=(i == 0), stop=(i == n - 1))
                return ins
            return S.add("pe", fn, reads, psr(bank))

        def tr_multi(bank, items, reads):
            items = list(items)

            def fn(e):
                ins = None
                for o, i_ in items:
                    ins = e.transpose(o, i_, ident[:])
                return ins
            return S.add("pe", fn, list(reads) + ["ident"], psr(bank))

        def act(out, in_, func, reads, writes, bias=0.0, scale=1.0, pbank=None):
            w = list(writes) + ([("pp", pbank)] if pbank is not None else [])
            r = list(reads) + ([("ps", pbank)] if pbank is not None else [])
            return S.add("act", lambda e: e.activation(out=out, in_=in_, func=func, bias=bias, scale=scale), r, w)

        def vec(eng, kind, reads, writes, pbank=None, **kw):
            w = list(writes) + ([("pp", pbank)] if pbank is not None else [])
            r = list(reads) + ([("ps", pbank)] if pbank is not None else [])
            if kind == "tt":
                fn = lambda e: e.tensor_tensor(out=kw["out"], in0=kw["in0"], in1=kw["in1"], op=kw["op"])
            elif kind == "stt":
                fn = lambda e: e.scalar_tensor_tensor(out=kw["out"], in0=kw["in0"], scalar=kw["scalar"],
                                                      in1=kw["in1"], op0=kw["op0"], op1=kw["op1"])
            elif kind == "ts":
                fn = lambda e: e.tensor_scalar(out=kw["out"], in0=kw["in0"], scalar1=kw["s1"],
                                               scalar2=kw.get("s2"), op0=kw["op0"],
                                               **({"op1": kw["op1"]} if "op1" in kw else {}))
            elif kind == "copy":
                fn = lambda e: e.tensor_copy(kw["out"], kw["in_"])
            elif kind == "memset":
                fn = lambda e: e.memset(kw["out"], kw["val"])
            elif kind == "bn_stats":
                fn = lambda e: e.bn_stats(kw["out"], kw["in_"])
            elif kind == "bn_aggr":
                fn = lambda e: e.bn_aggr(kw["out"], kw["in_"])
            else:
                raise ValueError(kind)
            return S.add(eng, fn, r, w)

        def dma(queue, out, in_, reads, writes, key):
            return S.add(queue, lambda e: e.dma_start(out=out, in_=in_), reads, writes, dma_key=key)


        def wreq(src3, nch):
            def fn(e, slot):
                return e.dma_start(out=wring[:, slot, 0:nch, :], in_=src3)
            j, slot, res = WR.request([fn])
            return j, wring[:, slot, 0:nch, :], res

        def w_in_blk(l, j):
            return w_in_d[l].rearrange("(c p) n -> p c n", p=128)[:, :, j * 128:(j + 1) * 128]

        def w_blk(wd, l, j):
            return wd[l].rearrange("(c p) n -> p c n", p=128)[:, :, j * 128:(j + 1) * 128]

        def xres(tb):
            return [("xTb", c, tb) for c in range(NCH)]

        def tbs(tb):
            return slice(tb * 512, (tb + 1) * 512)

        dma("sp", ident[:], ident_d, [], ["ident"], "const")
        dma("sp", chand[:, 0:257], chand_d, [], ["chand"], "const")
        dma("sp", binT[:], binT_d, [], ["binT"], "const")
        dma("sp", cvec[:], cvec_d, [], ["cvec"], "const")
        dma("sp", ovec[:], ovec_d, [], ["ovec"], "const")
        dma("sp", convw[:], cw_d, [], ["convw"], "const")
        vec("pool", "memset", [], ["onesb"], out=onesb[:], val=1.0)
        vec("pool", "memset", [], ["neghalf"], out=neghalf[:], val=-0.5)
        vec("pool", "memset", [], ["epsT"], out=epsT[:], val=EPS)
        vec("pool", "memset", [], ["bdiag"], out=bdiag[:], val=0.0)
        vec("pool", "memset", [], ["bdiag"], out=bdiag[0:64, 0:64], val=1.0)
        vec("pool", "memset", [], ["bdiag"], out=bdiag[64:128, 64:128], val=1.0)
        vec("dve", "copy", ["ident"], ["identb"], out=identb[:], in_=ident[:])
        vec("dve", "ts", ["cvec"], ["g8"], out=g8[:], in0=cvec[:, :, 1, :], s1=1.0, op0=ALU.mult)

        xin = [av(M_ + s * 4096, [128, DM]) for s in range(4)]
        for t in range(NTT):
            s = t % 4
            dma("sp", xin[s], x_d[t * 128:(t + 1) * 128, :], [], [("@M", "xin", s)] + ([("xgate",)] if t == 11 else []),
                f"xin{s}")
            for half in range(2):
                b = PS.next()
                tr_multi(b, [(psum[b][:, k * 128:(k + 1) * 128],
                              xin[s][:, (half * 4 + k) * 128:(half * 4 + k + 1) * 128]) for k in range(4)],
                         [("@M", "xin", s)])
                src = psum[b][:].rearrange("p (a b) -> p a b", a=4)
                cs = slice(half * 4, half * 4 + 4)
                act(xT32[:, cs, t * 128:(t + 1) * 128], src, AF.Copy, [],
                    [("x32", c, t // 4) for c in range(half * 4, half * 4 + 4)], pbank=b)
                vec("dve", "copy", [], [("xTb", c, t // 4) for c in range(half * 4, half * 4 + 4)],
                    pbank=b, out=xTb[:, cs, t * 128:(t + 1) * 128], in_=src)

        WR = Ring(S, "wr", RING, "pool")
        rings = [WR]

        st1a = {}
        zT_all = av(Y0, [128, 4, T], BF16)

        def emit_1a_tb(l, tb):
            if l not in st1a:
                st1a[l] = [wreq(w_in_blk(l, 12 + g), 8) for g in range(4)]
            for g in range(4):
                j, W, wres = st1a[l][g]
                b = PS.next()
                mm(b, psum[b][:], [(W[:, c, :], xTb[:, c, tbs(tb)]) for c in range(NCH)],
                   wres + xres(tb))
                WR.used(j)
                act(zT_all[:, g, tbs(tb)], psum[b][:], AF.Identity, ["binT"], [("@Y0", "z", g, tb)],
                    bias=binT[:, l, 12 + g:13 + g], pbank=b)

        pre1a = set()

        def layer(l, last):
            def bias(j):
                return binT[:, l, j:j + 1]

            S.fence(["Y0", "Y1", "Y2", "M", "TM"])
            zT = av(Y0, [128, 4, T], BF16)
            U = av(M_, [128, NTT, 1024], BF16)
            stage = [av(Y1 + s * 8192, [128, 2, 16, 128], BF16) for s in range(2)]
            wbg = av(Y2, [128, 4, 8, 128], BF16)
            ST = Ring(S, f"stg{l}_", 2, "sp", tag="@Y1")
            rings.append(ST)

            algt = av(TM, [128, 512])
            albt = av(TM + 2048, [128, 512])
            wsT = av(TM + 4096, [128, 8, 128], BF16)
            e2 = av(TM + 6144, [128, 4, 128])
            bavr = av(TM + 8192, [128, 512], BF16)
            onesr = av(TM + 9216, [128, 128], BF16)

            dma("sp", algt, alg_d[l].to_broadcast([128, 512]), [], [("@TM", "alg")], "p2a")
            dma("sp", albt, alb_d[l].to_broadcast([128, 512]), [], [("@TM", "alb")], "p2b")
            for h in range(8):
                dma("sp", e2[(h % 2) * 64:(h % 2) * 64 + 64, h // 2, :],
                    abs_d[l][0:1, h * 128:(h + 1) * 128].to_broadcast([64, 128]), [], [("@TM", "e2", h)], f"p2c{h}")
            bavt = av(TM + 9472, [128, 512])
            dma("sp", bavt, bav_d[l].to_broadcast([128, 512]), [], [("@TM", "bavr")], "p2d")
            vec("pool", "memset", [], [("@TM", "onesr")], out=onesr[0:1, :], val=1.0)
            wtmp = av(Y2 + 10240, [128, 8, 128])
            WT = [("@Y2", "wtmp")]
            dma("sp", wtmp, aws_d[l].rearrange("h q k -> q h k"), [], WT, "p2e")
            for half in range(2):
                b = PS.next()
                tr_multi(b, [(psum[b][:, k * 128:(k + 1) * 128], wtmp[:, half * 4 + k, :]) for k in range(4)],
                         WT)
                vec("dve", "copy", [], [("@TM", "wsT", half)], pbank=b, out=wsT[:, half * 4:half * 4 + 4, :],
                    in_=psum[b][:].rearrange("p (a b) -> p a b", a=4))

            for g in range(4):
                dma("pool", wbg[:, g], w_in_blk(l, 16 + g), [("xgate",)], [("@Y2", "wbg", g)], f"wbg{g}")
            if l not in pre1a:
                for tb in range(NTB):
                    emit_1a_tb(l, tb)
            k = 0
            for t in range(NTT):
                for gp in range(2):
                    b = PS.next()
                    mm_multi(b, [(psum[b][:, q * 256:(q + 1) * 256],
                                  [(zT[:, 2 * gp + q, t * 128:(t + 1) * 128], chand[:, 0:256])]) for q in range(2)],
                             [("@Y0", "z", 2 * gp, t // 4), ("@Y0", "z", 2 * gp + 1, t // 4), "chand"])
                    if k % 2 == 0:
                        act(U[:, t, gp * 512:(gp + 1) * 512], psum[b][:], AF.Copy, [],
                            [("@M", "U", t, gp)], pbank=b)
                    else:
                        vec("dve", "copy", [], [("@M", "U", t, gp)], pbank=b,
                            out=U[:, t, gp * 512:(gp + 1) * 512], in_=psum[b][:])
                    k += 1
            ybT = zT
            t1d = [[av(Y2 + (8192 if i < 2 else 14336) + (s * 2 + (i % 2)) * 512, [128, 128]) for i in range(4)]
                   for s in range(2)]

            def tbl(lo, hi):
                return sorted(set(range(lo // 512, (hi - 1) // 512 + 1)))
            it1d = 0
            for qb in range(8):
                q0 = qb * 128
                m0 = T - q0 - 127
                nm = 128 if qb > 0 else 127

                def mk(cs, q0=q0):
                    src = dft_d[cs].rearrange("(t p) q -> p t q", p=128)[:, :, q0:q0 + 128]
                    return lambda e, slot: e.dma_start(out=stage[slot][:, cs], in_=src)
                js, slot, sres = ST.request([mk(0), mk(1)])
                for g in range(4):
                    s2 = it1d % 2
                    it1d += 1
                    sg1, sg2, pim, t2 = t1d[s2]
                    RT = [("@Y2", "t1d", s2, i) for i in range(4)]
                    ba = PS.next()
                    mm_multi(ba, [(psum[ba][:, 0:128], [(wbg[:, g, c, :], xTb[:, c, q0:q0 + 128]) for c in range(NCH)]),
                                  (psum[ba][:, 128:128 + nm], [(wbg[:, g, c, :], xTb[:, c, m0:m0 + nm]) for c in range(NCH)])],
                             [("@Y2", "wbg", g)] + xres(qb // 4) + [r for tb_ in tbl(m0, m0 + nm) for r in xres(tb_)])
                    bb = PS.next()
                    mm_multi(bb, [(psum[bb][:, 0:128], [(U[:, t, g * 256:g * 256 + 128], stage[slot][:, 0, t, :]) for t in range(NTT)]),
                                  (psum[bb][:, 128:256], [(U[:, t, g * 256 + 128:g * 256 + 256], stage[slot][:, 1, t, :]) for t in range(NTT)])],
                             sres + [("@M", "U", t, g // 2) for t in range(NTT)])
                    ST.used(js)
                    act(sg1, psum[ba][:, 0:128], AF.Silu, ["binT"], [RT[0]], bias=bias(16 + g), pbank=ba)
                    act(sg2[:, 0:nm], psum[ba][:, 128:128 + nm], AF.Silu, ["binT"], [RT[1]], bias=bias(16 + g), pbank=ba)
                    act(pim, psum[bb][:, 128:256], AF.Copy, [], [RT[2]], pbank=bb)
                    vec("dve", "tt", [RT[2]], [RT[3]], pbank=bb, out=t2, in0=psum[bb][:, 0:128], in1=pim, op=ALU.subtract)
                    vec("dve", "tt", [RT[2]], [RT[2]], pbank=bb, out=pim, in0=psum[bb][:, 0:128], in1=pim, op=ALU.add)
                    vec("dve", "tt", [RT[2], RT[0]], [("@Y0", "yb", g, qb // 4)],
                        out=ybT[:, g, q0:q0 + 128], in0=sg1, in1=pim, op=ALU.mult)
                    vec("dve", "tt", [RT[3], RT[1]], [("@Y0", "yb", g, tb_) for tb_ in tbl(m0, m0 + nm)],
                        out=ybT[:, g, m0:m0 + nm], in0=sg2[:, 0:nm], in1=t2[:, 127:127 - nm:-1] if nm < 128 else t2[:, ::-1],
                        op=ALU.mult)
            bn = PS.next()
            mm_multi(bn, [(psum[bn][:, g:g + 1], [(U[:, t, g * 256:g * 256 + 128], chand[:, 256:257]) for t in range(NTT)])
                          for g in range(4)] +
                         [(psum[bn][:, 8 + g:9 + g], [(wbg[:, g, c, :], xTb[:, c, 1024:1025]) for c in range(NCH)])
                          for g in range(4)],
                     ["chand"] + [("@M", "U", t, gp) for t in range(NTT) for gp in range(2)] +
                     [("@Y2", "wbg", g) for g in range(4)] + xres(2))
            nyq = t1d[0][0]
            RN = ("@Y2", "t1d", 0, 0)
            for g in range(4):
                act(nyq[:, g:g + 1], psum[bn][:, 8 + g:9 + g], AF.Silu, ["binT"], [RN], bias=bias(16 + g), pbank=bn)
            vec("dve", "tt", [RN], [RN], pbank=bn, out=nyq[:, 4:8], in0=nyq[:, 0:4], in1=psum[bn][:, 0:4], op=ALU.mult)
            for g in range(4):
                vec("dve", "copy", [RN], [("@Y0", "yb", g, 2)], out=ybT[:, g, 1024:1025], in_=nyq[:, 4 + g:5 + g])

            S.fence(["Y1", "Y2", "M"])
            ycT = av(Y1, [128, 4, T], BF16)
            P3 = Y2
            hT = [av(P3 + s * 4160, [128, 2080], BF16) for s in range(2)]
            Dg = [av(P3 + 8320 + s * 7936, [128, 31, 128], BF16) for s in range(2)]
            o = P3 + 8320 + 2 * 7936
            s_t = [av(o + s * 2048, [128, 512]) for s in range(2)]; o += 4096
            c32 = [av(o + s * 2048, [128, 512]) for s in range(3)]; o += 6144
            c16 = [av(o + s * 1024, [128, 512], BF16) for s in range(3)]; o += 3072
            vE = [av(o + s * 2048, [128, 512]) for s in range(2)]; o += 4096
            sgc = [av(o + s * 2048, [128, 512]) for s in range(3)]; o += 6144
            R3 = "@Y2+M"
            for s in range(2):
                vec("pool", "memset", [], [(R3, "h", s, k) for k in range(4)], out=hT[s][:], val=0.0)

            tiles = []

            def buildD(cb):
                for j in range(31):
                    vec("dve", "ts", ["identb", "convw"], [(R3, "D", cb % 2)], out=Dg[cb % 2][:, j, :], in0=identb[:],
                        s1=convw[:, l, cb, j:j + 1], op0=ALU.mult)

            def stage0(cb):
                hs = cb % 2
                jg, Wg, rg = wreq(w_in_blk(l, 24 + cb), 8)
                jv, Wv, rv = wreq(w_in_blk(l, 20 + cb), 8)
                for tb in range(NTB):
                    b1 = PS.next()
                    mm(b1, psum[b1][:], [(Wg[:, c, :], xTb[:, c, tbs(tb)]) for c in range(NCH)], rg + xres(tb))
                    WR.used(jg)
                    b2 = PS.next()
                    mm(b2, psum[b2][:], [(Wv[:, c, :], xTb[:, c, tbs(tb)]) for c in range(NCH)], rv + xres(tb))
                    WR.used(jv)
                    s2 = tb % 2
                    act(s_t[s2], psum[b1][:], AF.Sigmoid, ["binT"], [(R3, "s", s2)], bias=bias(24 + cb), pbank=b1)
                    vec("dve", "stt", [(R3, "s", s2), "binT"], [(R3, "h", hs, tb)], pbank=b2,
                        out=hT[hs][:, 15 + tb * 512:15 + (tb + 1) * 512], in0=psum[b2][:],
                        scalar=bias(20 + cb), in1=s_t[s2], op0=ALU.add, op1=ALU.mult)

            def stageA(i, cb, tb, jc, Wc, rc):
                hs = cb % 2
                k3 = i % 3
                b = PS.next()
                hres = [(R3, "h", hs, t2) for t2 in range(max(0, tb - 1), min(NTB, tb + 2))]
                mm(b, psum[b][:], [(Dg[cb % 2][:, j, :], hT[hs][:, tb * 512 + j:tb * 512 + j + 512]) for j in range(31)],
                   [(R3, "D", cb % 2)] + hres)
                act(c32[k3], psum[b][:], AF.Identity, ["cvec"], [(R3, "c32", k3)], bias=cvec[:, l, 0, cb:cb + 1], pbank=b)
                vec("pool", "copy", [(R3, "c32", k3)], [(R3, "c16", k3)], out=c16[k3], in_=c32[k3])
                b2 = PS.next()
                mm(b2, psum[b2][:], [(Wc[:, c, :], xTb[:, c, tbs(tb)]) for c in range(NCH)], rc + xres(tb))
                WR.used(jc)
                act(sgc[k3], psum[b2][:], AF.Silu, ["binT"], [(R3, "sgc", k3)], bias=bias(28 + cb), pbank=b2)

            def stageB(i, cb, tb):
                k3 = i % 3
                b = PS.next()
                mm(b, psum[b][:], [(bdiag[:], c16[k3])], ["bdiag", (R3, "c16", k3)])
                vec("dve", "stt", [(R3, "c32", k3)], [(R3, "c32", k3)], pbank=b, out=c32[k3], in0=psum[b][:],
                    scalar=-1.0 / 64, in1=c32[k3], op0=ALU.mult, op1=ALU.add)
                act(c16[k3], c32[k3], AF.Square, [(R3, "c32", k3)], [(R3, "c16", k3)])

            def stageC(i, cb, tb):
                k3 = i % 3
                k2 = i % 2
                b = PS.next()
                mm(b, psum[b][:], [(bdiag[:], c16[k3])], ["bdiag", (R3, "c16", k3)])
                act(vE[k2], psum[b][:], AF.Sqrt, ["epsT"], [(R3, "vE", k2)], bias=epsT[:, 0:1], scale=1.0 / 64, pbank=b)
                S.add("dve", (lambda o_: lambda e: e.reciprocal(o_, o_))(vE[k2]), [(R3, "vE", k2)], [(R3, "vE", k2)])
                vec("dve", "tt", [(R3, "vE", k2), (R3, "c32", k3)], [(R3, "c32", k3)], out=c32[k3], in0=c32[k3],
                    in1=vE[k2], op=ALU.mult)
                act(c32[k3], c32[k3], AF.Silu, [(R3, "c32", k3), "g8", "cvec"], [(R3, "c32", k3)],
                    bias=cvec[:, l, 2, cb:cb + 1], scale=g8[:, l, cb:cb + 1])
                vec("dve", "tt", [(R3, "c32", k3), (R3, "sgc", k3)], [("@Y1", "yc", cb, tb)],
                    out=ycT[:, cb, tbs(tb)], in0=c32[k3], in1=sgc[k3], op=ALU.mult)

            i = 0
            pend = []
            buildD(0)
            for cb in range(4):
                stage0(cb)
                if cb + 1 < 4:
                    buildD(cb + 1)
                jc, Wc, rc = wreq(w_in_blk(l, 28 + cb), 8)
                for tb in range(NTB):
                    stageA(i, cb, tb, jc, Wc, rc)
                    pend.append((i, cb, tb))
                    if len(pend) >= 2:
                        stageB(*pend[-2])
                    if len(pend) >= 3:
                        stageC(*pend[-3])
                    i += 1
            stageB(*pend[-1])
            stageC(*pend[-2])
            stageC(*pend[-1])

            S.fence(["Y2", "M"])
            yaT = av(Y2, [128, 4, T], BF16)
            vn = av(M_, [128, NTT, 512], BF16)
            o = M_ + 16384
            gv = [av(o + s * 2048, [128, 512]) for s in range(2)]; o += 4096
            ut = [av(o + s * 2048, [128, 512]) for s in range(2)]; o += 4096
            sga = [av(o + s * 2048, [128, 512]) for s in range(2)]; o += 4096
            st6 = [av(o + s * 32, [128, 6]) for s in range(2)]; o += 64
            mvv = [av(o + s * 8, [128, 2]) for s in range(2)]; o += 16
            rsv = [av(o + s * 8, [128, 2]) for s in range(2)]; o += 16
            wav = [wreq(w_in_blk(l, 4 + j), 8) for j in range(4)]
            for t in range(NTT):
                b = PS.next()
                groups = []
                rd = [("@TM", "bavr"), ("@TM", "onesr")] + xres(t // 4)
                for j in range(4):
                    jj, W, wres = wav[j]
                    pr = [(xTb[:, c, t * 128:(t + 1) * 128], W[:, c, :]) for c in range(NCH)]
                    groups.append((psum[b][:, j * 128:(j + 1) * 128], pr))
                    rd += wres
                mm_multi(b, groups, rd)
                for j in range(4):
                    WR.used(wav[j][0])
                s2 = t % 2
                vec("dve", "tt", [("@TM", "bavr")], [("@M", "gv", s2)], pbank=b, out=gv[s2], in0=psum[b][:],
                    in1=bavt, op=ALU.add)
                act(gv[s2], gv[s2], AF.Gelu_apprx_tanh, [("@M", "gv", s2)], [("@M", "gv", s2)])
                vec("dve", "bn_stats", [("@M", "gv", s2)], [("@M", "st6", s2)], out=st6[s2], in_=gv[s2])
                vec("dve", "bn_aggr", [("@M", "st6", s2)], [("@M", "mv", s2)], out=mvv[s2], in_=st6[s2])
                vec("dve", "ts", [("@M", "mv", s2)], [("@M", "rs", s2)], out=rsv[s2][:, 0:1], in0=mvv[s2][:, 1:2],
                    s1=EPS, op0=ALU.add)
                vec("pool", "tt", [("@M", "rs", s2), "neghalf"], [("@M", "rs", s2)], out=rsv[s2][:, 0:1],
                    in0=rsv[s2][:, 0:1], in1=neghalf[:, 0:1], op=ALU.pow)
                vec("dve", "stt", [("@M", "gv", s2), ("@M", "mv", s2), ("@TM", "alg")], [("@M", "gv", s2)],
                    out=gv[s2], in0=gv[s2], scalar=mvv[s2][:, 0:1], in1=algt, op0=ALU.subtract, op1=ALU.mult)
                vec("dve", "stt", [("@M", "gv", s2), ("@M", "rs", s2), ("@TM", "alb")], [("@M", "vn", t)],
                    out=vn[:, t, :], in0=gv[s2], scalar=rsv[s2][:, 0:1], in1=albt, op0=ALU.mult, op1=ALU.add)

            it = 0
            for cb in range(4):
                ju, Wu, ru = wreq(w_in_blk(l, cb), 8)
                jg, Wg, rg = wreq(w_in_blk(l, 8 + cb), 8)
                for tp in range(2):
                    bu, bg, bm = [], [], []
                    for q in range(2):
                        tb = tp * 2 + q
                        b = PS.next(); bu.append(b)
                        mm(b, psum[b][:], [(Wu[:, c, :], xTb[:, c, tbs(tb)]) for c in range(NCH)], ru + xres(tb))
                        WR.used(ju)
                    for q in range(2):
                        tb = tp * 2 + q
                        b = PS.next(); bg.append(b)
                        mm(b, psum[b][:], [(Wg[:, c, :], xTb[:, c, tbs(tb)]) for c in range(NCH)], rg + xres(tb))
                        WR.used(jg)
                    for q in range(2):
                        tb = tp * 2 + q
                        b = PS.next(); bm.append(b)
                        groups = []
                        for n4 in range(4):
                            n = tb * 4 + n4
                            for hh in range(2):
                                h = 2 * cb + hh
                                groups.append((psum[b][hh * 64:(hh + 1) * 64, n4 * 128:(n4 + 1) * 128],
                                               [(vn[:, n, h * 64:(h + 1) * 64], wsT[:, h, :])]))
                        mm_multi(b, groups, [("@M", "vn", tb * 4 + n4) for n4 in range(4)] +
                                 [("@TM", "wsT", cb // 2)])
                    for q in range(2):
                        s2 = q
                        act(ut[s2], psum[bu[q]][:], AF.Gelu_apprx_tanh, ["binT"], [("@M", "ut", s2)],
                            bias=bias(cb), pbank=bu[q])
                    for q in range(2):
                        s2 = q
                        act(sga[s2], psum[bg[q]][:], AF.Silu, ["binT"], [("@M", "sga", s2)],
                            bias=bias(8 + cb), pbank=bg[q])
                    for q in range(2):
                        tb = tp * 2 + q
                        s2 = q
                        vec("dve", "tt", [("@M", "ut", s2), ("@M", "sga", s2)], [("@M", "ut", s2)],
                            out=ut[s2], in0=ut[s2], in1=sga[s2], op=ALU.mult)
                        vec("dve", "tt", [("@TM", "e2", 2 * cb), ("@TM", "e2", 2 * cb + 1)], [("@M", "sga", s2)],
                            pbank=bm[q], out=sga[s2].rearrange("p (a b) -> p a b", a=4),
                            in0=psum[bm[q]][:].rearrange("p (a b) -> p a b", a=4),
                            in1=e2[:, cb:cb + 1, :].to_broadcast([128, 4, 128]), op=ALU.add)
                        vec("dve", "tt", [("@M", "ut", s2), ("@M", "sga", s2)], [("@Y2", "ya", cb, tb)],
                            out=yaT[:, cb, tbs(tb)], in0=ut[s2], in1=sga[s2], op=ALU.mult)

            S.fence(["M", "TM"])
            mT = av(M_, [128, NCH, T], BF16)
            macc = [av(TM + tb * 2048, [128, 512]) for tb in range(NTB)]
            gt = [av(TM + 8192 + s * 2048, [128, 512]) for s in range(2)]
            ys = [yaT, ybT, ycT]
            ytag = [("@Y2", "ya"), ("@Y0", "yb"), ("@Y1", "yc")]
            wds = (wpa_d, wpb_d, wpc_d)
            it = 0
            for fb in range(8):
                for i in range(3):
                    jg, Wg, rg = wreq(w_in_blk(l, 32 + i * 8 + fb), 8)
                    jp, Wp, rp = wreq(w_blk(wds[i], l, fb), 4)
                    for tb in range(NTB):
                        s2 = it % 2
                        it += 1
                        b1 = PS.next()
                        mm(b1, psum[b1][:], [(Wg[:, c, :], xTb[:, c, tbs(tb)]) for c in range(NCH)], rg + xres(tb))
                        WR.used(jg)
                        b2 = PS.next()
                        mm(b2, psum[b2][:], [(Wp[:, c, :], ys[i][:, c, tbs(tb)]) for c in range(4)],
                           rp + [ytag[i] + (c, tb) for c in range(4)])
                        WR.used(jp)
                        act(gt[s2], psum[b1][:], AF.Sigmoid, ["binT"], [("@TM", "gt", s2)],
                            bias=bias(32 + i * 8 + fb), pbank=b1)
                        if i == 0:
                            vec("dve", "tt", [("@TM", "gt", s2)], [("@TM", "macc", tb)], pbank=b2,
                                out=macc[tb], in0=gt[s2], in1=psum[b2][:], op=ALU.mult)
                        else:
                            vec("dve", "tt", [("@TM", "gt", s2)], [("@TM", "gt", s2)], pbank=b2,
                                out=gt[s2], in0=gt[s2], in1=psum[b2][:], op=ALU.mult)
                            if i == 1:
                                vec("pool", "tt", [("@TM", "gt", s2), ("@TM", "macc", tb)], [("@TM", "macc", tb)],
                                    out=macc[tb], in0=macc[tb], in1=gt[s2], op=ALU.add)
                            else:
                                vec("pool", "tt", [("@TM", "gt", s2), ("@TM", "macc", tb)], [("@M", "mT", fb, tb)],
                                    out=mT[:, fb, tbs(tb)], in0=macc[tb], in1=gt[s2], op=ALU.add)

            S.fence(["Y0", "Y1", "Y2", "TM"])
            vbf = av(Y0, [128, NCH, T], BF16)
            VB = "@Y0+Y1"
            pT = av(Y2, [128, 2, T], BF16)
            sg5 = [av(TM + s * 2048, [128, 512]) for s in range(2)]
            mx5 = [av(TM + 4096 + s * 2048, [128, 512]) for s in range(2)]
            pst = av(Y2 + 8192, [128, 8, 256])
            for hh in range(2):
                dma("sp", pst, p_d[l, hh * 1024:(hh + 1) * 1024, :].rearrange("(t p) f -> p t f", p=128), [],
                    [("@Y2", "pst")], "pst")
                for t8 in range(8):
                    t = hh * 8 + t8
                    b = PS.next()
                    tr_multi(b, [(psum[b][:, k * 128:(k + 1) * 128], pst[:, t8, k * 128:(k + 1) * 128]) for k in range(2)],
                             [("@Y2", "pst")])
                    vec("dve", "copy", [], [("@Y2", "pT", t // 4)], pbank=b, out=pT[:, :, t * 128:(t + 1) * 128],
                        in_=psum[b][:, 0:256].rearrange("p (a b) -> p a b", a=2))
            it = 0
            for ob in range(8):
                jo, Wo, ro = wreq(w_blk(wout_d, l, ob), 8)
                jq, Wq, rq = wreq(w_in_blk(l, 56 + ob), 8)
                je, We, re_ = wreq(w_blk(wple_d, l, ob), 2)
                for tb in range(NTB):
                    s2 = it % 2
                    it += 1
                    b1 = PS.next()
                    mm(b1, psum[b1][:], [(Wo[:, c, :], mT[:, c, tbs(tb)]) for c in range(NCH)],
                       ro + [("@M", "mT", c, tb) for c in range(NCH)])
                    WR.used(jo)
                    b2 = PS.next()
                    mm(b2, psum[b2][:], [(Wq[:, c, :], xTb[:, c, tbs(tb)]) for c in range(NCH)], rq + xres(tb))
                    WR.used(jq)
                    b3 = PS.next()
                    mm(b3, psum[b3][:], [(We[:, c, :], pT[:, c, tbs(tb)]) for c in range(2)],
                       re_ + [("@Y2", "pT", tb)])
                    WR.used(je)
                    act(sg5[s2], psum[b2][:], AF.Sigmoid, ["binT"], [("@TM", "sg5", s2)], bias=bias(56 + ob), pbank=b2)
                    act(mx5[s2], psum[b1][:], AF.Identity, ["ovec"], [("@TM", "mx5", s2)],
                        bias=ovec[:, l, 0, ob:ob + 1], pbank=b1)
                    vec("dve", "stt", [("x32", ob, tb), ("@TM", "mx5", s2)], [("@TM", "mx5", s2)], out=mx5[s2],
                        in0=xT32[:, ob, tbs(tb)], scalar=ALPHA, in1=mx5[s2], op0=ALU.mult, op1=ALU.add)
                    vec("dve", "tt", [("@TM", "sg5", s2)], [("@TM", "sg5", s2)], pbank=b3, out=sg5[s2],
                        in0=sg5[s2], in1=psum[b3][:], op=ALU.mult)
                    vec("dve", "tt", [("@TM", "sg5", s2), ("@TM", "mx5", s2)], [("x32", ob, tb)],
                        out=xT32[:, ob, tbs(tb)], in0=mx5[s2], in1=sg5[s2], op=ALU.add)
                    vec("pool", "copy", [("x32", ob, tb)], [(VB, "vb", ob, tb)], out=vbf[:, ob, tbs(tb)],
                        in_=xT32[:, ob, tbs(tb)])

            S.fence(["TM", "Y2"])
            sq6 = [av(Y2 + 10240 + s * 1024, [128, 512], BF16) for s in range(2)]
            vE6 = [av(Y2 + 12288 + s * 2048, [128, 512]) for s in range(2)]
            ost = [av(TM + s * 4096, [128, DM]) for s in range(2)]
            cnt6 = [0]

            def stageX(tb):
                mb = PS.next()
                mm(mb, psum[mb][:], [(onesb[:], vbf[:, c, tbs(tb)]) for c in range(NCH)],
                   ["onesb"] + [(VB, "vb", c, tb) for c in range(NCH)])
                bv = PS.next()
                for c in range(NCH):
                    s2 = cnt6[0] % 2
                    cnt6[0] += 1
                    vec("dve", "stt", [("x32", c, tb)], [("x32", c, tb)], pbank=mb, out=xT32[:, c, tbs(tb)],
                        in0=psum[mb][:], scalar=-1.0 / DM, in1=xT32[:, c, tbs(tb)], op0=ALU.mult, op1=ALU.add)
                    act(sq6[s2], xT32[:, c, tbs(tb)], AF.Square, [("x32", c, tb)], [("@Y2", "sq6", s2)])
                    mm(bv, psum[bv][:], [(onesb[:], sq6[s2])], ["onesb", ("@Y2", "sq6", s2)],
                       start=(c == 0), stop=(c == NCH - 1))
                k2 = tb % 2
                act(vE6[k2], psum[bv][:], AF.Sqrt, ["epsT"], [("@Y2", "vE6", k2)], bias=epsT[:, 0:1],
                    scale=1.0 / DM, pbank=bv)
                S.add("dve", (lambda o_: lambda e: e.reciprocal(o_, o_))(vE6[k2]), [("@Y2", "vE6", k2)],
                      [("@Y2", "vE6", k2)])

            def stageY(tb):
                k2 = tb % 2
                for c in range(NCH):
                    vec("dve", "tt", [("x32", c, tb), ("@Y2", "vE6", k2)], [("x32", c, tb)],
                        out=xT32[:, c, tbs(tb)], in0=xT32[:, c, tbs(tb)], in1=vE6[k2], op=ALU.mult)
                    act(xT32[:, c, tbs(tb)], xT32[:, c, tbs(tb)], AF.Identity, [("x32", c, tb), "ovec"],
                        [("x32", c, tb)], bias=ovec[:, l, 2, c:c + 1], scale=ovec[:, l, 1, c:c + 1])
                    if not last:
                        vec("dve", "copy", [("x32", c, tb)], [("xTb", c, tb)], out=xTb[:, c, tbs(tb)],
                            in_=xT32[:, c, tbs(tb)])
                if last:
                    for t4 in range(4):
                        t = tb * 4 + t4
                        s2 = t % 2
                        for half in range(2):
                            b = PS.next()
                            tr_multi(b, [(psum[b][:, k * 128:(k + 1) * 128],
                                          xT32[:, half * 4 + k, t * 128:(t + 1) * 128]) for k in range(4)],
                                     [("x32", half * 4 + k, tb) for k in range(4)])
                            if half == 0:
                                act(ost[s2][:, 0:512], psum[b][:], AF.Copy, [], [("@TM", "ost", s2, 0)], pbank=b)
                            else:
                                vec("dve", "copy", [], [("@TM", "ost", s2, 1)], pbank=b, out=ost[s2][:, 512:1024],
                                    in_=psum[b][:])
                        dma("sp", out_d[t * 128:(t + 1) * 128, :], ost[s2],
                            [("@TM", "ost", s2, 0), ("@TM", "ost", s2, 1)], [], f"out{s2}")
                else:
                    emit_1a_tb(l + 1, tb)
                    pre1a.add(l + 1)

            stageX(0)
            for tb in range(NTB):
                if tb + 1 < NTB:
                    stageX(tb + 1)
                stageY(tb)

        for l in range(NL):
            layer(l, l == NL - 1)
        for rg_ in rings:
            rg_.finalize()
        S.emit(nc, final_wait_keys=["out0", "out1"])
    return nc


_CONST = {}


def _consts():
    if not _CONST:
        n = np.arange(T, dtype=np.float64)
        ang = 2.0 * np.pi * ((n[:, None] * n[None, :]) % T) / T
        dft = np.stack([np.cos(ang), np.sin(ang)]) / 512.0
        _CONST["dft"] = dft.astype(np.float32).astype(ml_dtypes.bfloat16)
        c = np.arange(128, dtype=np.float64)
        a2 = 2.0 * np.pi * ((c[:, None] * c[None, :]) % 128) / 128
        alt = (((-1.0) ** c) / 512.0)[:, None]
        _CONST["chand"] = np.concatenate([np.cos(a2), -np.sin(a2), alt], axis=1).astype(np.float32).astype(ml_dtypes.bfloat16)
        _CONST["ident"] = np.eye(128, dtype=np.float32)
    return _CONST


_NC_CACHE = {}


def _prep_shared(inp):
    f = lambda a: np.ascontiguousarray(np.asarray(a, dtype=np.float32))
    c = _consts()
    sh = {
        "w_in": f(inp["w_in"]),
        "binT": f(np.transpose(np.asarray(inp["b_in"]).reshape(DEPTH, 64, 128), (2, 0, 1))),
        "bav": f(np.asarray(inp["b_in"])[:, None, 512:1024]),
        "alg": f(np.asarray(inp["a_ln_g"])[:, None, :]),
        "alb": f(np.asarray(inp["a_ln_b"])[:, None, :]),
        "aws": f(inp["a_ws"]),
        "absr": f(np.asarray(inp["a_bs"]).reshape(DEPTH, 1, 1024)),
        "convw": f(np.transpose(np.asarray(inp["c_conv_w"]).reshape(DEPTH, 31, 4, 128), (3, 0, 2, 1))),
        "cvec": f(np.transpose(np.stack([np.asarray(inp[k]).reshape(DEPTH, 4, 128)
                                         for k in ("c_conv_b", "c_ln_g", "c_ln_b")], axis=1), (3, 0, 1, 2))),
        "w_pa": f(inp["w_pa"]), "w_pb": f(inp["w_pb"]), "w_pc": f(inp["w_pc"]),
        "w_out": f(inp["w_out"]), "w_ple": f(inp["w_ple"]),
        "ovec": f(np.transpose(np.stack([np.asarray(inp[k]).reshape(DEPTH, 8, 128)
                                         for k in ("b_out", "ln_g", "ln_b")], axis=1), (3, 0, 1, 2))),
        "ident": c["ident"], "chand": c["chand"], "dft": c["dft"],
    }
    return sh


def kernel(**inputs):
    x = np.asarray(inputs["x"], dtype=np.float32)
    p = np.asarray(inputs["p"], dtype=np.float32)
    B = x.shape[0]
    sh = _prep_shared(inputs)
    if "nc" not in _NC_CACHE:
        _NC_CACHE["nc"] = build(DEPTH)
    nc = _NC_CACHE["nc"]
    in_maps = []
    for b in range(B):
        m = dict(sh)
        m["x"] = np.ascontiguousarray(x[b])
        m["p"] = np.ascontiguousarray(p[:, b])
        in_maps.append(m)
    res = run_bass_kernel_spmd(nc, in_maps, core_ids=list(range(B)))
    return np.stack([np.asarray(r["out"], dtype=np.float32) for r in res.results], axis=0)
```

```python
import numpy as np
import ml_dtypes
from contextlib import ExitStack

import concourse.bass as bass
import concourse.mybir as mybir
from concourse.bass_utils import run_bass_kernel_spmd

F32 = mybir.dt.float32
BF16 = mybir.dt.bfloat16
AF = mybir.ActivationFunctionType
ALU = mybir.AluOpType

T = 2048
DM = 1024
NCH = 8
NTB = 4
NTT = 16
DEPTH = 2
ALPHA = float((2 * DEPTH) ** 0.25)
EPS = 1e-5
RING = 7


class Op:
    __slots__ = ("eng", "fn", "reads", "writes", "dma_key", "waits", "marked",
                 "sig", "pos", "fence")

    def __init__(self, eng, fn, reads, writes, dma_key, pos):
        self.eng = eng
        self.fn = fn
        self.reads = reads
        self.writes = writes
        self.dma_key = dma_key
        self.waits = {}
        self.marked = False
        self.sig = None
        self.pos = pos
        self.fence = None

    def stream(self):
        return ("dma_" + self.dma_key) if self.dma_key is not None else ("eng_" + self.eng)


def _tags(r):
    t = r[0] if isinstance(r, tuple) else r
    if isinstance(t, str) and t.startswith("@"):
        return t[1:].split("+")
    return ()


class Sched:
    def __init__(self):
        self.ops = []
        self.n = 0

    def add(self, eng, fn, reads=(), writes=(), dma_key=None, pos=None):
        if pos is None:
            self.n += 1
            pos = float(self.n)
        op = Op(eng, fn, tuple(reads), tuple(writes), dma_key, pos)
        self.ops.append(op)
        return op

    def fence(self, tags):
        self.n += 1
        op = Op(None, None, (), (), None, float(self.n))
        op.fence = tuple(tags)
        self.ops.append(op)

    def cur(self):
        return float(self.n)

    def analyze(self):
        self.ops.sort(key=lambda o: o.pos)
        for i, op in enumerate(self.ops):
            op.pos = i
        last_w = {}
        readers = {}
        reg_acc = {}
        reg_fence = {}
        for op in self.ops:
            if op.fence is not None:
                for t in op.fence:
                    best = {}
                    for o in reg_acc.get(t, []) + reg_fence.get(t, []):
                        s = o.stream()
                        if s not in best or best[s].pos < o.pos:
                            best[s] = o
                    reg_fence[t] = list(best.values())
                    reg_acc[t] = []
                continue
            deps = {}
            def consider(d, raw):
                if d is op:
                    return
                d_async = d.dma_key is not None
                if (not d_async) and d.eng == op.eng and op.dma_key is None and not raw and op.eng != "pool":
                    return
                s = d.stream()
                cur = deps.get(s)
                if cur is None or cur.pos < d.pos:
                    deps[s] = d
            for r in op.reads:
                w = last_w.get(r)
                if w is not None:
                    consider(w, True)
            for r in op.writes:
                w = last_w.get(r)
                if w is not None:
                    consider(w, False)
                for rd in readers.get(r, ()):
                    consider(rd, False)
            for r in op.reads + op.writes:
                for t in _tags(r):
                    for d in reg_fence.get(t, ()):
                        consider(d, True)
                    lst = reg_acc.setdefault(t, [])
                    if not lst or lst[-1] is not op:
                        lst.append(op)
            for s, d in deps.items():
                d.marked = True
                op.waits[s] = d
            for r in op.reads:
                readers.setdefault(r, []).append(op)
            for r in op.writes:
                last_w[r] = op
                readers[r] = []
            for r in op.reads:
                lst = readers[r]
                if len(lst) > 12:
                    best = {}
                    for o in lst:
                        s = o.stream()
                        if s not in best or best[s].pos < o.pos:
                            best[s] = o
                    readers[r] = sorted(best.values(), key=lambda o: o.pos)
        cnt = {}
        for op in self.ops:
            if op.fence is not None:
                continue
            if op.dma_key is not None:
                key = op.stream()
                cnt[key] = cnt.get(key, 0) + 16
                op.sig = (key, cnt[key])
                op.marked = True
            elif op.marked:
                key = op.stream()
                cnt[key] = cnt.get(key, 0) + 1
                op.sig = (key, cnt[key])
        for op in self.ops:
            if op.dma_key is not None and op.dma_key.startswith("const"):
                op.sig = (op.sig[0], cnt[op.sig[0]])
        self.final = cnt

    def emit(self, nc, final_wait_keys=()):
        self.analyze()
        with ExitStack() as es:
            sems = {k: es.enter_context(nc.semaphore(k)) for k in sorted(self.final)}
            block = es.enter_context(nc.Block())

            def run(engname):
                def body(e):
                    waited = {}
                    for op in self.ops:
                        if op.eng != engname:
                            continue
                        for s, d in op.waits.items():
                            k, v = d.sig
                            if waited.get(k, 0) < v:
                                e.wait_ge(sems[k], v)
                                waited[k] = v
                        ins = op.fn(e)
                        if op.marked:
                            k, v = op.sig
                            ins.then_inc(sems[k], 16 if op.dma_key is not None else 1)
                    if engname == "sp":
                        for k in final_wait_keys:
                            kk = "dma_" + k
                            if kk in self.final:
                                e.wait_ge(sems[kk], self.final[kk])
                return body

            block.tensor(run("pe"))
            block.scalar(run("act"))
            block.vector(run("dve"))
            block.gpsimd(run("pool"))
            block.sync(run("sp"))


class Ring:
    def __init__(self, S, name, nslots, queue, tag=None):
        self.S = S
        self.name = name
        self.n = nslots
        self.queue = queue
        self.tag = tag
        self.start = S.cur()
        self.reqs = []

    def res(self, slot, k):
        if self.tag:
            return (self.tag, self.name, slot, k)
        return (self.name, slot, k)

    def request(self, dma_fns):
        j = len(self.reqs)
        slot = j % self.n
        self.reqs.append([dma_fns, None, None, self.S.cur()])
        return j, slot, [self.res(slot, k) for k in range(len(dma_fns))]

    def used(self, j):
        self.reqs[j][1] = self.S.cur()
        if self.reqs[j][2] is None:
            self.reqs[j][2] = self.S.cur()

    def finalize(self):
        for j, (fns, _, first, reqpos) in enumerate(self.reqs):
            slot = j % self.n
            if j < self.n:
                base = self.start + 1e-4 * (j + 1)
            else:
                lp = self.reqs[j - self.n][1]
                assert lp is not None, (self.name, j)
                base = lp + 0.3 + 1e-4 * j
            assert first is not None and base < first, (self.name, j, base, first)
            for k, fn in enumerate(fns):
                self.S.add(self.queue, (lambda fn, slot: lambda e: fn(e, slot))(fn, slot),
                           reads=[("xgate",)], writes=[self.res(slot, k)],
                           dma_key=f"{self.name}{slot}_{k}", pos=base + 1e-6 * k)


def build(NL=DEPTH):
    nc = bass.Bass("TRN2", target_bir_lowering=False)

    def dram(n, s, dt=F32, kind="ExternalInput"):
        return nc.dram_tensor(n, list(s), dt, kind=kind).ap()

    x_d = dram("x", [T, DM])
    p_d = dram("p", [DEPTH, T, 256])
    w_in_d = dram("w_in", [DEPTH, DM, 8192])
    binT_d = dram("binT", [128, DEPTH, 64])
    bav_d = dram("bav", [DEPTH, 1, 512])
    alg_d = dram("alg", [DEPTH, 1, 512])
    alb_d = dram("alb", [DEPTH, 1, 512])
    aws_d = dram("aws", [DEPTH, 8, 128, 128])
    abs_d = dram("absr", [DEPTH, 1, 1024])
    cw_d = dram("convw", [128, DEPTH, 4, 31])
    cvec_d = dram("cvec", [128, DEPTH, 3, 4])
    wpa_d = dram("w_pa", [DEPTH, 512, DM])
    wpb_d = dram("w_pb", [DEPTH, 512, DM])
    wpc_d = dram("w_pc", [DEPTH, 512, DM])
    wout_d = dram("w_out", [DEPTH, DM, DM])
    wple_d = dram("w_ple", [DEPTH, 256, DM])
    ovec_d = dram("ovec", [128, DEPTH, 3, 8])
    ident_d = dram("ident", [128, 128])
    chand_d = dram("chand", [128, 257], BF16)
    dft_d = dram("dft", [2, T, T], BF16)
    out_d = dram("out", [T, DM], F32, kind="ExternalOutput")

    S = Sched()
    es = ExitStack()
    with es:
        def sb(n, s, dt=F32):
            return es.enter_context(nc.sbuf_tensor(n, list(s), dt))

        xT32 = sb("xT32", [128, NCH, T])
        xTb = sb("xTb", [128, NCH, T], BF16)
        wring = sb("wring", [128, RING, 8, 128], BF16)
        ident = sb("ident_s", [128, 128])
        identb = sb("identb", [128, 128], BF16)
        onesb = sb("onesb", [128, 128], BF16)
        bdiag = sb("bdiag", [128, 128], BF16)
        chand = sb("chand_s", [128, 258], BF16)
        neghalf = sb("neghalf", [128, 2])
        binT = sb("binT_s", [128, DEPTH, 64])
        cvec = sb("cvec_s", [128, DEPTH, 3, 4])
        ovec = sb("ovec_s", [128, DEPTH, 3, 8])
        convw = sb("convw_s", [128, DEPTH, 4, 31])
        g8 = sb("g8", [128, DEPTH, 4])
        epsT = sb("epsT", [128, 1])
        AR_BYTES = 92 * 1024
        arena = sb("arena", [128, AR_BYTES // 4])
        psum = [es.enter_context(nc.psum_tensor(f"ps{i}", [128, 512], F32)) for i in range(8)]

        Y0, Y1, Y2, M_, TM = 0, 16384, 32768, 49152, 81920

        def av(off, shape, dt=F32):
            esz = 4 if dt == F32 else 2
            nel = int(np.prod(shape[1:]))
            nbytes = nel * esz
            assert off % 4 == 0 and nbytes % 4 == 0 and off + nbytes <= AR_BYTES, (off, shape)
            v = arena[:, off // 4:(off + nbytes) // 4]
            if dt != F32:
                v = v.bitcast(dt)
            if len(shape) == 3:
                v = v.rearrange("p (a b) -> p a b", a=shape[1])
            elif len(shape) == 4:
                v = v.rearrange("p (a b c) -> p a b c", a=shape[1], b=shape[2])
            return v

        class PS:
            reserved = set()
            i = 0

            @classmethod
            def next(cls):
                while True:
                    b = cls.i % 8
                    cls.i += 1
                    if b not in cls.reserved:
                        return b

        def psr(b):
            return [("ps", b), ("pp", b)]

        def mm(bank, out_ap, pairs, reads, start=True, stop=True, extra_w=()):
            pairs = list(pairs)

            def fn(e):
                ins = None
                n = len(pairs)
                for i, (l, r) in enumerate(pairs):
                    ins = e.matmul(out_ap, l, r, start=(start and i == 0), stop=(stop and i == n - 1))
                return ins
            return S.add("pe", fn, reads, psr(bank) + list(extra_w))

        def mm_multi(bank, groups, reads):
            groups = [(o, list(pr)) for o, pr in groups]

            def fn(e):
                ins = None
                for o, pr in groups:
                    n = len(pr)
                    for i, (l, r) in enumerate(pr):
                        ins = e.matmul(o, l, r, start=(i == 0), stop=(i == n - 1))
                return ins
            return S.add("pe", fn, reads, psr(bank))

        def tr_multi(bank, items, reads):
            items = list(items)

            def fn(e):
                ins = None
                for o, i_ in items:
                    ins = e.transpose(o, i_, ident[:])
                return ins
            return S.add("pe", fn, list(reads) + ["ident"], psr(bank))

        def act(out, in_, func, reads, writes, bias=0.0, scale=1.0, pbank=None):
            w = list(writes) + ([("pp", pbank)] if pbank is not None else [])
            r = list(reads) + ([("ps", pbank)] if pbank is not None else [])
            return S.add("act", lambda e: e.activation(out=out, in_=in_, func=func, bias=bias, scale=scale), r, w)

        def vec(eng, kind, reads, writes, pbank=None, **kw):
            w = list(writes) + ([("pp", pbank)] if pbank is not None else [])
            r = list(reads) + ([("ps", pbank)] if pbank is not None else [])
            if kind == "tt":
                fn = lambda e: e.tensor_tensor(out=kw["out"], in0=kw["in0"], in1=kw["in1"], op=kw["op"])
            elif kind == "stt":
                fn = lambda e: e.scalar_tensor_tensor(out=kw["out"], in0=kw["in0"], scalar=kw["scalar"],
                                                      in1=kw["in1"], op0=kw["op0"], op1=kw["op1"])
            elif kind == "ts":
                fn = lambda e: e.tensor_scalar(out=kw["out"], in0=kw["in0"], scalar1=kw["s1"],
                                               scalar2=kw.get("s2"), op0=kw["op0"],
                                               **({"op1": kw["op1"]} if "op1" in kw else {}))
            elif kind == "copy":
                fn = lambda e: e.tensor_copy(kw["out"], kw["in_"])
            elif kind == "memset":
                fn = lambda e: e.memset(kw["out"], kw["val"])
            elif kind == "bn_stats":
                fn = lambda e: e.bn_stats(kw["out"], kw["in_"])
            elif kind == "bn_aggr":
                fn = lambda e: e.bn_aggr(kw["out"], kw["in_"])
            else:
                raise ValueError(kind)
            return S.add(eng, fn, r, w)

        def dma(queue, out, in_, reads, writes, key):
            return S.add(queue, lambda e: e.dma_start(out=out, in_=in_), reads, writes, dma_key=key)


        def wreq(src3, nch):
            def fn(e, slot):
                return e.dma_start(out=wring[:, slot, 0:nch, :], in_=src3)
            j, slot, res = WR.request([fn])
            return j, wring[:, slot, 0:nch, :], res

        def w_in_blk(l, j):
            return w_in_d[l].rearrange("(c p) n -> p c n", p=128)[:, :, j * 128:(j + 1) * 128]

        def w_blk(wd, l, j):
            return wd[l].rearrange("(c p) n -> p c n", p=128)[:, :, j * 128:(j + 1) * 128]

        def xres(tb):
            return [("xTb", c, tb) for c in range(NCH)]

        def tbs(tb):
            return slice(tb * 512, (tb + 1) * 512)

        dma("sp", ident[:], ident_d, [], ["ident"], "const")
        dma("sp", chand[:, 0:257], chand_d, [], ["chand"], "const")
        dma("sp", binT[:], binT_d, [], ["binT"], "const")
        dma("sp", cvec[:], cvec_d, [], ["cvec"], "const")
        dma("sp", ovec[:], ovec_d, [], ["ovec"], "const")
        dma("sp", convw[:], cw_d, [], ["convw"], "const")
        vec("pool", "memset", [], ["onesb"], out=onesb[:], val=1.0)
        vec("pool", "memset", [], ["neghalf"], out=neghalf[:], val=-0.5)
        vec("pool", "memset", [], ["epsT"], out=epsT[:], val=EPS)
        vec("pool", "memset", [], ["bdiag"], out=bdiag[:], val=0.0)
        vec("pool", "memset", [], ["bdiag"], out=bdiag[0:64, 0:64], val=1.0)
        vec("pool", "memset", [], ["bdiag"], out=bdiag[64:128, 64:128], val=1.0)
        vec("dve", "copy", ["ident"], ["identb"], out=identb[:], in_=ident[:])
        vec("dve", "ts", ["cvec"], ["g8"], out=g8[:], in0=cvec[:, :, 1, :], s1=1.0, op0=ALU.mult)

        xin = [av(M_ + s * 4096, [128, DM]) for s in range(4)]
        for t in range(NTT):
            s = t % 4
            dma("sp", xin[s], x_d[t * 128:(t + 1) * 128, :], [], [("@M", "xin", s)] + ([("xgate",)] if t == 11 else []),
                f"xin{s}")
            for half in range(2):
                b = PS.next()
                tr_multi(b, [(psum[b][:, k * 128:(k + 1) * 128],
                              xin[s][:, (half * 4 + k) * 128:(half * 4 + k + 1) * 128]) for k in range(4)],
                         [("@M", "xin", s)])
                src = psum[b][:].rearrange("p (a b) -> p a b", a=4)
                cs = slice(half * 4, half * 4 + 4)
                act(xT32[:, cs, t * 128:(t + 1) * 128], src, AF.Copy, [],
                    [("x32", c, t // 4) for c in range(half * 4, half * 4 + 4)], pbank=b)
                vec("dve", "copy", [], [("xTb", c, t // 4) for c in range(half * 4, half * 4 + 4)],
                    pbank=b, out=xTb[:, cs, t * 128:(t + 1) * 128], in_=src)

        WR = Ring(S, "wr", RING, "pool")
        rings = [WR]

        st1a = {}
        zT_all = av(Y0, [128, 4, T], BF16)

        def emit_1a_tb(l, tb):
            if l not in st1a:
                st1a[l] = [wreq(w_in_blk(l, 12 + g), 8) for g in range(4)]
            for g in range(4):
                j, W, wres = st1a[l][g]
                b = PS.next()
                mm(b, psum[b][:], [(W[:, c, :], xTb[:, c, tbs(tb)]) for c in range(NCH)],
                   wres + xres(tb))
                WR.used(j)
                act(zT_all[:, g, tbs(tb)], psum[b][:], AF.Identity, ["binT"], [("@Y0", "z", g, tb)],
                    bias=binT[:, l, 12 + g:13 + g], pbank=b)

        pre1a = set()

        def layer(l, last):
            def bias(j):
                return binT[:, l, j:j + 1]

            S.fence(["Y0", "Y1", "Y2", "M", "TM"])
            zT = av(Y0, [128, 4, T], BF16)
            U = av(M_, [128, NTT, 1024], BF16)
            stage = [av(Y1 + s * 8192, [128, 2, 16, 128], BF16) for s in range(2)]
            wbg = av(Y2, [128, 4, 8, 128], BF16)
            ST = Ring(S, f"stg{l}_", 2, "sp", tag="@Y1")
            rings.append(ST)

            algt = av(TM, [128, 512])
            albt = av(TM + 2048, [128, 512])
            wsT = av(TM + 4096, [128, 8, 128], BF16)
            e2 = av(TM + 6144, [128, 4, 128])
            bavr = av(TM + 8192, [128, 512], BF16)
            onesr = av(TM + 9216, [128, 128], BF16)

            dma("sp", algt, alg_d[l].to_broadcast([128, 512]), [], [("@TM", "alg")], "p2a")
            dma("sp", albt, alb_d[l].to_broadcast([128, 512]), [], [("@TM", "alb")], "p2b")
            for h in range(8):
                dma("sp", e2[(h % 2) * 64:(h % 2) * 64 + 64, h // 2, :],
                    abs_d[l][0:1, h * 128:(h + 1) * 128].to_broadcast([64, 128]), [], [("@TM", "e2", h)], f"p2c{h}")
            dma("pool", bavr[0:1, :], bav_d[l], [], [("@TM", "bavr")], "p2d")
            vec("pool", "memset", [], [("@TM", "onesr")], out=onesr[0:1, :], val=1.0)
            wtmp = av(Y2 + 10240, [128, 8, 128])
            WT = [("@Y2", "wtmp")]
            dma("sp", wtmp, aws_d[l].rearrange("h q k -> q h k"), [], WT, "p2e")
            for half in range(2):
                b = PS.next()
                tr_multi(b, [(psum[b][:, k * 128:(k + 1) * 128], wtmp[:, half * 4 + k, :]) for k in range(4)],
                         WT)
                vec("dve", "copy", [], [("@TM", "wsT", half)], pbank=b, out=wsT[:, half * 4:half * 4 + 4, :],
                    in_=psum[b][:].rearrange("p (a b) -> p a b", a=4))

            for g in range(4):
                dma("pool", wbg[:, g], w_in_blk(l, 16 + g), [("xgate",)], [("@Y2", "wbg", g)], f"wbg{g}")
            if l not in pre1a:
                for tb in range(NTB):
                    emit_1a_tb(l, tb)
            k = 0
            for t in range(NTT):
                for gp in range(2):
                    b = PS.next()
                    mm_multi(b, [(psum[b][:, q * 256:(q + 1) * 256],
                                  [(zT[:, 2 * gp + q, t * 128:(t + 1) * 128], chand[:, 0:256])]) for q in range(2)],
                             [("@Y0", "z", 2 * gp, t // 4), ("@Y0", "z", 2 * gp + 1, t // 4), "chand"])
                    if k % 2 == 0:
                        act(U[:, t, gp * 512:(gp + 1) * 512], psum[b][:], AF.Copy, [],
                            [("@M", "U", t, gp)], pbank=b)
                    else:
                        vec("dve", "copy", [], [("@M", "U", t, gp)], pbank=b,
                            out=U[:, t, gp * 512:(gp + 1) * 512], in_=psum[b][:])
                    k += 1
            ybT = zT
            t1d = [[av(Y2 + (8192 if i < 2 else 14336) + (s * 2 + (i % 2)) * 512, [128, 128]) for i in range(4)]
                   for s in range(2)]

            def tbl(lo, hi):
                return sorted(set(range(lo // 512, (hi - 1) // 512 + 1)))
            it1d = 0
            for qb in range(8):
                q0 = qb * 128
                m0 = T - q0 - 127
                nm = 128 if qb > 0 else 127

                def mk(cs, q0=q0):
                    src = dft_d[cs].rearrange("(t p) q -> p t q", p=128)[:, :, q0:q0 + 128]
                    return lambda e, slot: e.dma_start(out=stage[slot][:, cs], in_=src)
                js, slot, sres = ST.request([mk(0), mk(1)])
                for g in range(4):
                    s2 = it1d % 2
                    it1d += 1
                    sg1, sg2, pim, t2 = t1d[s2]
                    RT = [("@Y2", "t1d", s2, i) for i in range(4)]
                    ba = PS.next()
                    mm_multi(ba, [(psum[ba][:, 0:128], [(wbg[:, g, c, :], xTb[:, c, q0:q0 + 128]) for c in range(NCH)]),
                                  (psum[ba][:, 128:128 + nm], [(wbg[:, g, c, :], xTb[:, c, m0:m0 + nm]) for c in range(NCH)])],
                             [("@Y2", "wbg", g)] + xres(qb // 4) + [r for tb_ in tbl(m0, m0 + nm) for r in xres(tb_)])
                    bb = PS.next()
                    mm_multi(bb, [(psum[bb][:, 0:128], [(U[:, t, g * 256:g * 256 + 128], stage[slot][:, 0, t, :]) for t in range(NTT)]),
                                  (psum[bb][:, 128:256], [(U[:, t, g * 256 + 128:g * 256 + 256], stage[slot][:, 1, t, :]) for t in range(NTT)])],
                             sres + [("@M", "U", t, g // 2) for t in range(NTT)])
                    ST.used(js)
                    act(sg1, psum[ba][:, 0:128], AF.Silu, ["binT"], [RT[0]], bias=bias(16 + g), pbank=ba)
                    act(sg2[:, 0:nm], psum[ba][:, 128:128 + nm], AF.Silu, ["binT"], [RT[1]], bias=bias(16 + g), pbank=ba)
                    act(pim, psum[bb][:, 128:256], AF.Copy, [], [RT[2]], pbank=bb)
                    vec("dve", "tt", [RT[2]], [RT[3]], pbank=bb, out=t2, in0=psum[bb][:, 0:128], in1=pim, op=ALU.subtract)
                    vec("dve", "tt", [RT[2]], [RT[2]], pbank=bb, out=pim, in0=psum[bb][:, 0:128], in1=pim, op=ALU.add)
                    vec("dve", "tt", [RT[2], RT[0]], [("@Y0", "yb", g, qb // 4)],
                        out=ybT[:, g, q0:q0 + 128], in0=sg1, in1=pim, op=ALU.mult)
                    vec("dve", "tt", [RT[3], RT[1]], [("@Y0", "yb", g, tb_) for tb_ in tbl(m0, m0 + nm)],
                        out=ybT[:, g, m0:m0 + nm], in0=sg2[:, 0:nm], in1=t2[:, 127:127 - nm:-1] if nm < 128 else t2[:, ::-1],
                        op=ALU.mult)
            bn = PS.next()
            mm_multi(bn, [(psum[bn][:, g:g + 1], [(U[:, t, g * 256:g * 256 + 128], chand[:, 256:257]) for t in range(NTT)])
                          for g in range(4)] +
                         [(psum[bn][:, 8 + g:9 + g], [(wbg[:, g, c, :], xTb[:, c, 1024:1025]) for c in range(NCH)])
                          for g in range(4)],
                     ["chand"] + [("@M", "U", t, gp) for t in range(NTT) for gp in range(2)] +
                     [("@Y2", "wbg", g) for g in range(4)] + xres(2))
            nyq = t1d[0][0]
            RN = ("@Y2", "t1d", 0, 0)
            for g in range(4):
                act(nyq[:, g:g + 1], psum[bn][:, 8 + g:9 + g], AF.Silu, ["binT"], [RN], bias=bias(16 + g), pbank=bn)
            vec("dve", "tt", [RN], [RN], pbank=bn, out=nyq[:, 4:8], in0=nyq[:, 0:4], in1=psum[bn][:, 0:4], op=ALU.mult)
            for g in range(4):
                vec("dve", "copy", [RN], [("@Y0", "yb", g, 2)], out=ybT[:, g, 1024:1025], in_=nyq[:, 4 + g:5 + g])

            S.fence(["Y1", "Y2", "M"])
            ycT = av(Y1, [128, 4, T], BF16)
            P3 = Y2
            hT = [av(P3 + s * 4160, [128, 2080], BF16) for s in range(2)]
            Dg = [av(P3 + 8320 + s * 7936, [128, 31, 128], BF16) for s in range(2)]
            o = P3 + 8320 + 2 * 7936
            s_t = [av(o + s * 2048, [128, 512]) for s in range(2)]; o += 4096
            c32 = [av(o + s * 2048, [128, 512]) for s in range(3)]; o += 6144
            c16 = [av(o + s * 1024, [128, 512], BF16) for s in range(3)]; o += 3072
            vE = [av(o + s * 2048, [128, 512]) for s in range(2)]; o += 4096
            sgc = [av(o + s * 2048, [128, 512]) for s in range(3)]; o += 6144
            R3 = "@Y2+M"
            for s in range(2):
                vec("pool", "memset", [], [(R3, "h", s, k) for k in range(4)], out=hT[s][:], val=0.0)

            tiles = []

            def buildD(cb):
                for j in range(31):
                    vec("dve", "ts", ["identb", "convw"], [(R3, "D", cb % 2)], out=Dg[cb % 2][:, j, :], in0=identb[:],
                        s1=convw[:, l, cb, j:j + 1], op0=ALU.mult)

            def stage0(cb):
                hs = cb % 2
                jg, Wg, rg = wreq(w_in_blk(l, 24 + cb), 8)
                jv, Wv, rv = wreq(w_in_blk(l, 20 + cb), 8)
                for tb in range(NTB):
                    b1 = PS.next()
                    mm(b1, psum[b1][:], [(Wg[:, c, :], xTb[:, c, tbs(tb)]) for c in range(NCH)], rg + xres(tb))
                    WR.used(jg)
                    b2 = PS.next()
                    mm(b2, psum[b2][:], [(Wv[:, c, :], xTb[:, c, tbs(tb)]) for c in range(NCH)], rv + xres(tb))
                    WR.used(jv)
                    s2 = tb % 2
                    act(s_t[s2], psum[b1][:], AF.Sigmoid, ["binT"], [(R3, "s", s2)], bias=bias(24 + cb), pbank=b1)
                    vec("dve", "stt", [(R3, "s", s2), "binT"], [(R3, "h", hs, tb)], pbank=b2,
                        out=hT[hs][:, 15 + tb * 512:15 + (tb + 1) * 512], in0=psum[b2][:],
                        scalar=bias(20 + cb), in1=s_t[s2], op0=ALU.add, op1=ALU.mult)

            def stageA(i, cb, tb, jc, Wc, rc):
                hs = cb % 2
                k3 = i % 3
                b = PS.next()
                hres = [(R3, "h", hs, t2) for t2 in range(max(0, tb - 1), min(NTB, tb + 2))]
                mm(b, psum[b][:], [(Dg[cb % 2][:, j, :], hT[hs][:, tb * 512 + j:tb * 512 + j + 512]) for j in range(31)],
                   [(R3, "D", cb % 2)] + hres)
                act(c32[k3], psum[b][:], AF.Identity, ["cvec"], [(R3, "c32", k3)], bias=cvec[:, l, 0, cb:cb + 1], pbank=b)
                vec("pool", "copy", [(R3, "c32", k3)], [(R3, "c16", k3)], out=c16[k3], in_=c32[k3])
                b2 = PS.next()
                mm(b2, psum[b2][:], [(Wc[:, c, :], xTb[:, c, tbs(tb)]) for c in range(NCH)], rc + xres(tb))
                WR.used(jc)
                act(sgc[k3], psum[b2][:], AF.Silu, ["binT"], [(R3, "sgc", k3)], bias=bias(28 + cb), pbank=b2)

            def stageB(i, cb, tb):
                k3 = i % 3
                b = PS.next()
                mm(b, psum[b][:], [(bdiag[:], c16[k3])], ["bdiag", (R3, "c16", k3)])
                vec("dve", "stt", [(R3, "c32", k3)], [(R3, "c32", k3)], pbank=b, out=c32[k3], in0=psum[b][:],
                    scalar=-1.0 / 64, in1=c32[k3], op0=ALU.mult, op1=ALU.add)
                act(c16[k3], c32[k3], AF.Square, [(R3, "c32", k3)], [(R3, "c16", k3)])

            def stageC(i, cb, tb):
                k3 = i % 3
                k2 = i % 2
                b = PS.next()
                mm(b, psum[b][:], [(bdiag[:], c16[k3])], ["bdiag", (R3, "c16", k3)])
                act(vE[k2], psum[b][:], AF.Sqrt, ["epsT"], [(R3, "vE", k2)], bias=epsT[:, 0:1], scale=1.0 / 64, pbank=b)
                S.add("dve", (lambda o_: lambda e: e.reciprocal(o_, o_))(vE[k2]), [(R3, "vE", k2)], [(R3, "vE", k2)])
                vec("dve", "tt", [(R3, "vE", k2), (R3, "c32", k3)], [(R3, "c32", k3)], out=c32[k3], in0=c32[k3],
                    in1=vE[k2], op=ALU.mult)
                act(c32[k3], c32[k3], AF.Silu, [(R3, "c32", k3), "g8", "cvec"], [(R3, "c32", k3)],
                    bias=cvec[:, l, 2, cb:cb + 1], scale=g8[:, l, cb:cb + 1])
                vec("dve", "tt", [(R3, "c32", k3), (R3, "sgc", k3)], [("@Y1", "yc", cb, tb)],
                    out=ycT[:, cb, tbs(tb)], in0=c32[k3], in1=sgc[k3], op=ALU.mult)

            i = 0
            pend = []
            buildD(0)
            for cb in range(4):
                stage0(cb)
                if cb + 1 < 4:
                    buildD(cb + 1)
                jc, Wc, rc = wreq(w_in_blk(l, 28 + cb), 8)
                for tb in range(NTB):
                    stageA(i, cb, tb, jc, Wc, rc)
                    pend.append((i, cb, tb))
                    if len(pend) >= 2:
                        stageB(*pend[-2])
                    if len(pend) >= 3:
                        stageC(*pend[-3])
                    i += 1
            stageB(*pend[-1])
            stageC(*pend[-2])
            stageC(*pend[-1])

            S.fence(["Y2", "M"])
            yaT = av(Y2, [128, 4, T], BF16)
            vn = av(M_, [128, NTT, 512], BF16)
            o = M_ + 16384
            gv = [av(o + s * 2048, [128, 512]) for s in range(2)]; o += 4096
            ut = [av(o + s * 2048, [128, 512]) for s in range(2)]; o += 4096
            sga = [av(o + s * 2048, [128, 512]) for s in range(2)]; o += 4096
            st6 = [av(o + s * 32, [128, 6]) for s in range(2)]; o += 64
            mvv = [av(o + s * 8, [128, 2]) for s in range(2)]; o += 16
            rsv = [av(o + s * 8, [128, 2]) for s in range(2)]; o += 16
            wav = [wreq(w_in_blk(l, 4 + j), 8) for j in range(4)]
            for t in range(NTT):
                b = PS.next()
                groups = []
                rd = [("@TM", "bavr"), ("@TM", "onesr")] + xres(t // 4)
                for j in range(4):
                    jj, W, wres = wav[j]
                    pr = [(xTb[:, c, t * 128:(t + 1) * 128], W[:, c, :]) for c in range(NCH)]
                    pr.append((onesr[0:1, :], bavr[0:1, j * 128:(j + 1) * 128]))
                    groups.append((psum[b][:, j * 128:(j + 1) * 128], pr))
                    rd += wres
                mm_multi(b, groups, rd)
                for j in range(4):
                    WR.used(wav[j][0])
                s2 = t % 2
                act(gv[s2], psum[b][:], AF.Gelu_apprx_tanh, [], [("@M", "gv", s2)], pbank=b)
                vec("dve", "bn_stats", [("@M", "gv", s2)], [("@M", "st6", s2)], out=st6[s2], in_=gv[s2])
                vec("dve", "bn_aggr", [("@M", "st6", s2)], [("@M", "mv", s2)], out=mvv[s2], in_=st6[s2])
                vec("dve", "ts", [("@M", "mv", s2)], [("@M", "rs", s2)], out=rsv[s2][:, 0:1], in0=mvv[s2][:, 1:2],
                    s1=EPS, op0=ALU.add)
                vec("pool", "tt", [("@M", "rs", s2), "neghalf"], [("@M", "rs", s2)], out=rsv[s2][:, 0:1],
                    in0=rsv[s2][:, 0:1], in1=neghalf[:, 0:1], op=ALU.pow)
                vec("dve", "stt", [("@M", "gv", s2), ("@M", "mv", s2), ("@TM", "alg")], [("@M", "gv", s2)],
                    out=gv[s2], in0=gv[s2], scalar=mvv[s2][:, 0:1], in1=algt, op0=ALU.subtract, op1=ALU.mult)
                vec("dve", "stt", [("@M", "gv", s2), ("@M", "rs", s2), ("@TM", "alb")], [("@M", "vn", t)],
                    out=vn[:, t, :], in0=gv[s2], scalar=rsv[s2][:, 0:1], in1=albt, op0=ALU.mult, op1=ALU.add)

            it = 0
            for cb in range(4):
                ju, Wu, ru = wreq(w_in_blk(l, cb), 8)
                jg, Wg, rg = wreq(w_in_blk(l, 8 + cb), 8)
                for tp in range(2):
                    bu, bg, bm = [], [], []
                    for q in range(2):
                        tb = tp * 2 + q
                        b = PS.next(); bu.append(b)
                        mm(b, psum[b][:], [(Wu[:, c, :], xTb[:, c, tbs(tb)]) for c in range(NCH)], ru + xres(tb))
                        WR.used(ju)
                    for q in range(2):
                        tb = tp * 2 + q
                        b = PS.next(); bg.append(b)
                        mm(b, psum[b][:], [(Wg[:, c, :], xTb[:, c, tbs(tb)]) for c in range(NCH)], rg + xres(tb))
                        WR.used(jg)
                    for q in range(2):
                        tb = tp * 2 + q
                        b = PS.next(); bm.append(b)
                        groups = []
                        for n4 in range(4):
                            n = tb * 4 + n4
                            for hh in range(2):
                                h = 2 * cb + hh
                                groups.append((psum[b][hh * 64:(hh + 1) * 64, n4 * 128:(n4 + 1) * 128],
                                               [(vn[:, n, h * 64:(h + 1) * 64], wsT[:, h, :])]))
                        mm_multi(b, groups, [("@M", "vn", tb * 4 + n4) for n4 in range(4)] +
                                 [("@TM", "wsT", cb // 2)])
                    for q in range(2):
                        s2 = q
                        act(ut[s2], psum[bu[q]][:], AF.Gelu_apprx_tanh, ["binT"], [("@M", "ut", s2)],
                            bias=bias(cb), pbank=bu[q])
                    for q in range(2):
                        s2 = q
                        act(sga[s2], psum[bg[q]][:], AF.Silu, ["binT"], [("@M", "sga", s2)],
                            bias=bias(8 + cb), pbank=bg[q])
                    for q in range(2):
                        tb = tp * 2 + q
                        s2 = q
                        vec("dve", "tt", [("@M", "ut", s2), ("@M", "sga", s2)], [("@M", "ut", s2)],
                            out=ut[s2], in0=ut[s2], in1=sga[s2], op=ALU.mult)
                        vec("dve", "tt", [("@TM", "e2", 2 * cb), ("@TM", "e2", 2 * cb + 1)], [("@M", "sga", s2)],
                            pbank=bm[q], out=sga[s2].rearrange("p (a b) -> p a b", a=4),
                            in0=psum[bm[q]][:].rearrange("p (a b) -> p a b", a=4),
                            in1=e2[:, cb:cb + 1, :].to_broadcast([128, 4, 128]), op=ALU.add)
                        vec("dve", "tt", [("@M", "ut", s2), ("@M", "sga", s2)], [("@Y2", "ya", cb, tb)],
                            out=yaT[:, cb, tbs(tb)], in0=ut[s2], in1=sga[s2], op=ALU.mult)

            S.fence(["M", "TM"])
            mT = av(M_, [128, NCH, T], BF16)
            macc = [av(TM + tb * 2048, [128, 512]) for tb in range(NTB)]
            gt = [av(TM + 8192 + s * 2048, [128, 512]) for s in range(2)]
            ys = [yaT, ybT, ycT]
            ytag = [("@Y2", "ya"), ("@Y0", "yb"), ("@Y1", "yc")]
            wds = (wpa_d, wpb_d, wpc_d)
            it = 0
            for fb in range(8):
                for i in range(3):
                    jg, Wg, rg = wreq(w_in_blk(l, 32 + i * 8 + fb), 8)
                    jp, Wp, rp = wreq(w_blk(wds[i], l, fb), 4)
                    for tb in range(NTB):
                        s2 = it % 2
                        it += 1
                        b1 = PS.next()
                        mm(b1, psum[b1][:], [(Wg[:, c, :], xTb[:, c, tbs(tb)]) for c in range(NCH)], rg + xres(tb))
                        WR.used(jg)
                        b2 = PS.next()
                        mm(b2, psum[b2][:], [(Wp[:, c, :], ys[i][:, c, tbs(tb)]) for c in range(4)],
                           rp + [ytag[i] + (c, tb) for c in range(4)])
                        WR.used(jp)
                        act(gt[s2], psum[b1][:], AF.Sigmoid, ["binT"], [("@TM", "gt", s2)],
                            bias=bias(32 + i * 8 + fb), pbank=b1)
                        if i == 0:
                            vec("dve", "tt", [("@TM", "gt", s2)], [("@TM", "macc", tb)], pbank=b2,
                                out=macc[tb], in0=gt[s2], in1=psum[b2][:], op=ALU.mult)
                        else:
                            vec("dve", "tt", [("@TM", "gt", s2)], [("@TM", "gt", s2)], pbank=b2,
                                out=gt[s2], in0=gt[s2], in1=psum[b2][:], op=ALU.mult)
                            if i == 1:
                                vec("pool", "tt", [("@TM", "gt", s2), ("@TM", "macc", tb)], [("@TM", "macc", tb)],
                                    out=macc[tb], in0=macc[tb], in1=gt[s2], op=ALU.add)
                            else:
                                vec("pool", "tt", [("@TM", "gt", s2), ("@TM", "macc", tb)], [("@M", "mT", fb, tb)],
                                    out=mT[:, fb, tbs(tb)], in0=macc[tb], in1=gt[s2], op=ALU.add)

            S.fence(["Y0", "Y1", "Y2", "TM"])
            vbf = av(Y0, [128, NCH, T], BF16)
            VB = "@Y0+Y1"
            pT = av(Y2, [128, 2, T], BF16)
            sg5 = [av(TM + s * 2048, [128, 512]) for s in range(2)]
            mx5 = [av(TM + 4096 + s * 2048, [128, 512]) for s in range(2)]
            pst = av(Y2 + 8192, [128, 8, 256])
            for hh in range(2):
                dma("sp", pst, p_d[l, hh * 1024:(hh + 1) * 1024, :].rearrange("(t p) f -> p t f", p=128), [],
                    [("@Y2", "pst")], "pst")
                for t8 in range(8):
                    t = hh * 8 + t8
                    b = PS.next()
                    tr_multi(b, [(psum[b][:, k * 128:(k + 1) * 128], pst[:, t8, k * 128:(k + 1) * 128]) for k in range(2)],
                             [("@Y2", "pst")])
                    vec("dve", "copy", [], [("@Y2", "pT", t // 4)], pbank=b, out=pT[:, :, t * 128:(t + 1) * 128],
                        in_=psum[b][:, 0:256].rearrange("p (a b) -> p a b", a=2))
            it = 0
            for ob in range(8):
                jo, Wo, ro = wreq(w_blk(wout_d, l, ob), 8)
                jq, Wq, rq = wreq(w_in_blk(l, 56 + ob), 8)
                je, We, re_ = wreq(w_blk(wple_d, l, ob), 2)
                for tb in range(NTB):
                    s2 = it % 2
                    it += 1
                    b1 = PS.next()
                    mm(b1, psum[b1][:], [(Wo[:, c, :], mT[:, c, tbs(tb)]) for c in range(NCH)],
                       ro + [("@M", "mT", c, tb) for c in range(NCH)])
                    WR.used(jo)
                    b2 = PS.next()
                    mm(b2, psum[b2][:], [(Wq[:, c, :], xTb[:, c, tbs(tb)]) for c in range(NCH)], rq + xres(tb))
                    WR.used(jq)
                    b3 = PS.next()
                    mm(b3, psum[b3][:], [(We[:, c, :], pT[:, c, tbs(tb)]) for c in range(2)],
                       re_ + [("@Y2", "pT", tb)])
                    WR.used(je)
                    act(sg5[s2], psum[b2][:], AF.Sigmoid, ["binT"], [("@TM", "sg5", s2)], bias=bias(56 + ob), pbank=b2)
                    act(mx5[s2], psum[b1][:], AF.Identity, ["ovec"], [("@TM", "mx5", s2)],
                        bias=ovec[:, l, 0, ob:ob + 1], pbank=b1)
                    vec("dve", "stt", [("x32", ob, tb), ("@TM", "mx5", s2)], [("@TM", "mx5", s2)], out=mx5[s2],
                        in0=xT32[:, ob, tbs(tb)], scalar=ALPHA, in1=mx5[s2], op0=ALU.mult, op1=ALU.add)
                    vec("dve", "tt", [("@TM", "sg5", s2)], [("@TM", "sg5", s2)], pbank=b3, out=sg5[s2],
                        in0=sg5[s2], in1=psum[b3][:], op=ALU.mult)
                    vec("dve", "tt", [("@TM", "sg5", s2), ("@TM", "mx5", s2)], [("x32", ob, tb)],
                        out=xT32[:, ob, tbs(tb)], in0=mx5[s2], in1=sg5[s2], op=ALU.add)
                    vec("pool", "copy", [("x32", ob, tb)], [(VB, "vb", ob, tb)], out=vbf[:, ob, tbs(tb)],
                        in_=xT32[:, ob, tbs(tb)])

            S.fence(["TM", "Y2"])
            sq6 = [av(Y2 + 10240 + s * 1024, [128, 512], BF16) for s in range(2)]
            vE6 = [av(Y2 + 12288 + s * 2048, [128, 512]) for s in range(2)]
            ost = [av(TM + s * 4096, [128, DM]) for s in range(2)]
            cnt6 = [0]

            def stageX(tb):
                mb = PS.next()
                mm(mb, psum[mb][:], [(onesb[:], vbf[:, c, tbs(tb)]) for c in range(NCH)],
                   ["onesb"] + [(VB, "vb", c, tb) for c in range(NCH)])
                bv = PS.next()
                for c in range(NCH):
                    s2 = cnt6[0] % 2
                    cnt6[0] += 1
                    vec("dve", "stt", [("x32", c, tb)], [("x32", c, tb)], pbank=mb, out=xT32[:, c, tbs(tb)],
                        in0=psum[mb][:], scalar=-1.0 / DM, in1=xT32[:, c, tbs(tb)], op0=ALU.mult, op1=ALU.add)
                    act(sq6[s2], xT32[:, c, tbs(tb)], AF.Square, [("x32", c, tb)], [("@Y2", "sq6", s2)])
                    mm(bv, psum[bv][:], [(onesb[:], sq6[s2])], ["onesb", ("@Y2", "sq6", s2)],
                       start=(c == 0), stop=(c == NCH - 1))
                k2 = tb % 2
                act(vE6[k2], psum[bv][:], AF.Sqrt, ["epsT"], [("@Y2", "vE6", k2)], bias=epsT[:, 0:1],
                    scale=1.0 / DM, pbank=bv)
                S.add("dve", (lambda o_: lambda e: e.reciprocal(o_, o_))(vE6[k2]), [("@Y2", "vE6", k2)],
                      [("@Y2", "vE6", k2)])

            def stageY(tb):
                k2 = tb % 2
                for c in range(NCH):
                    vec("dve", "tt", [("x32", c, tb), ("@Y2", "vE6", k2)], [("x32", c, tb)],
                        out=xT32[:, c, tbs(tb)], in0=xT32[:, c, tbs(tb)], in1=vE6[k2], op=ALU.mult)
                    act(xT32[:, c, tbs(tb)], xT32[:, c, tbs(tb)], AF.Identity, [("x32", c, tb), "ovec"],
                        [("x32", c, tb)], bias=ovec[:, l, 2, c:c + 1], scale=ovec[:, l, 1, c:c + 1])
                    if not last:
                        vec("dve", "copy", [("x32", c, tb)], [("xTb", c, tb)], out=xTb[:, c, tbs(tb)],
                            in_=xT32[:, c, tbs(tb)])
                if last:
                    for t4 in range(4):
                        t = tb * 4 + t4
                        s2 = t % 2
                        for half in range(2):
                            b = PS.next()
                            tr_multi(b, [(psum[b][:, k * 128:(k + 1) * 128],
                                          xT32[:, half * 4 + k, t * 128:(t + 1) * 128]) for k in range(4)],
                                     [("x32", half * 4 + k, tb) for k in range(4)])
                            if half == 0:
                                act(ost[s2][:, 0:512], psum[b][:], AF.Copy, [], [("@TM", "ost", s2, 0)], pbank=b)
                            else:
                                vec("dve", "copy", [], [("@TM", "ost", s2, 1)], pbank=b, out=ost[s2][:, 512:1024],
                                    in_=psum[b][:])
                        dma("sp", out_d[t * 128:(t + 1) * 128, :], ost[s2],
                            [("@TM", "ost", s2, 0), ("@TM", "ost", s2, 1)], [], f"out{s2}")
                else:
                    emit_1a_tb(l + 1, tb)
                    pre1a.add(l + 1)

            stageX(0)
            for tb in range(NTB):
                if tb + 1 < NTB:
                    stageX(tb + 1)
                stageY(tb)

        for l in range(NL):
            layer(l, l == NL - 1)
        for rg_ in rings:
            rg_.finalize()
        S.emit(nc, final_wait_keys=["out0", "out1"])
    return nc


_CONST = {}


def _consts():
    if not _CONST:
        n = np.arange(T, dtype=np.float64)
        ang = 2.0 * np.pi * ((n[:, None] * n[None, :]) % T) / T
        dft = np.stack([np.cos(ang), np.sin(ang)]) / 512.0
        _CONST["dft"] = dft.astype(np.float32).astype(ml_dtypes.bfloat16)
        c = np.arange(128, dtype=np.float64)
        a2 = 2.0 * np.pi * ((c[:, None] * c[None, :]) % 128) / 128
        alt = (((-1.0) ** c) / 512.0)[:, None]
        _CONST["chand"] = np.concatenate([np.cos(a2), -np.sin(a2), alt], axis=1).astype(np.float32).astype(ml_dtypes.bfloat16)
        _CONST["ident"] = np.eye(128, dtype=np.float32)
    return _CONST


_NC_CACHE = {}


def _prep_shared(inp):
    f = lambda a: np.ascontiguousarray(np.asarray(a, dtype=np.float32))
    c = _consts()
    sh = {
        "w_in": f(inp["w_in"]),
        "binT": f(np.transpose(np.asarray(inp["b_in"]).reshape(DEPTH, 64, 128), (2, 0, 1))),
        "bav": f(np.asarray(inp["b_in"])[:, None, 512:1024]),
        "alg": f(np.asarray(inp["a_ln_g"])[:, None, :]),
        "alb": f(np.asarray(inp["a_ln_b"])[:, None, :]),
        "aws": f(inp["a_ws"]),
        "absr": f(np.asarray(inp["a_bs"]).reshape(DEPTH, 1, 1024)),
        "convw": f(np.transpose(np.asarray(inp["c_conv_w"]).reshape(DEPTH, 31, 4, 128), (3, 0, 2, 1))),
        "cvec": f(np.transpose(np.stack([np.asarray(inp[k]).reshape(DEPTH, 4, 128)
                                         for k in ("c_conv_b", "c_ln_g", "c_ln_b")], axis=1), (3, 0, 1, 2))),
        "w_pa": f(inp["w_pa"]), "w_pb": f(inp["w_pb"]), "w_pc": f(inp["w_pc"]),
        "w_out": f(inp["w_out"]), "w_ple": f(inp["w_ple"]),
        "ovec": f(np.transpose(np.stack([np.asarray(inp[k]).reshape(DEPTH, 8, 128)
                                         for k in ("b_out", "ln_g", "ln_b")], axis=1), (3, 0, 1, 2))),
        "ident": c["ident"], "chand": c["chand"], "dft": c["dft"],
    }
    return sh


def kernel(**inputs):
    x = np.asarray(inputs["x"], dtype=np.float32)
    p = np.asarray(inputs["p"], dtype=np.float32)
    B = x.shape[0]
    sh = _prep_shared(inputs)
    if "nc" not in _NC_CACHE:
        _NC_CACHE["nc"] = build(DEPTH)
    nc = _NC_CACHE["nc"]
    in_maps = []
    for b in range(B):
        m = dict(sh)
        m["x"] = np.ascontiguousarray(x[b])
        m["p"] = np.ascontiguousarray(p[:, b])
        in_maps.append(m)
    res = run_bass_kernel_spmd(nc, in_maps, core_ids=list(range(B)))
    return np.stack([np.asarray(r["out"], dtype=np.float32) for r in res.results], axis=0)
```

```python
import numpy as np
import ml_dtypes
from contextlib import ExitStack

import concourse.bass as bass
import concourse.mybir as mybir
from concourse.bass_utils import run_bass_kernel_spmd

F32 = mybir.dt.float32
BF16 = mybir.dt.bfloat16
AF = mybir.ActivationFunctionType
ALU = mybir.AluOpType

T = 2048
DM = 1024
NCH = 8
NTB = 4
NTT = 16
DEPTH = 2
ALPHA = float((2 * DEPTH) ** 0.25)
EPS = 1e-5
RING = 7


class Op:
    __slots__ = ("eng", "fn", "reads", "writes", "dma_key", "waits", "marked",
                 "sig", "pos", "fence")

    def __init__(self, eng, fn, reads, writes, dma_key, pos):
        self.eng = eng
        self.fn = fn
        self.reads = reads
        self.writes = writes
        self.dma_key = dma_key
        self.waits = {}
        self.marked = False
        self.sig = None
        self.pos = pos
        self.fence = None

    def stream(self):
        return ("dma_" + self.dma_key) if self.dma_key is not None else ("eng_" + self.eng)


def _tags(r):
    t = r[0] if isinstance(r, tuple) else r
    if isinstance(t, str) and t.startswith("@"):
        return t[1:].split("+")
    return ()


class Sched:
    def __init__(self):
        self.ops = []
        self.n = 0

    def add(self, eng, fn, reads=(), writes=(), dma_key=None, pos=None):
        if pos is None:
            self.n += 1
            pos = float(self.n)
        op = Op(eng, fn, tuple(reads), tuple(writes), dma_key, pos)
        self.ops.append(op)
        return op

    def fence(self, tags):
        self.n += 1
        op = Op(None, None, (), (), None, float(self.n))
        op.fence = tuple(tags)
        self.ops.append(op)

    def cur(self):
        return float(self.n)

    def analyze(self):
        self.ops.sort(key=lambda o: o.pos)
        for i, op in enumerate(self.ops):
            op.pos = i
        last_w = {}
        readers = {}
        reg_acc = {}
        reg_fence = {}
        for op in self.ops:
            if op.fence is not None:
                for t in op.fence:
                    best = {}
                    for o in reg_acc.get(t, []) + reg_fence.get(t, []):
                        s = o.stream()
                        if s not in best or best[s].pos < o.pos:
                            best[s] = o
                    reg_fence[t] = list(best.values())
                    reg_acc[t] = []
                continue
            deps = {}
            def consider(d, raw):
                if d is op:
                    return
                d_async = d.dma_key is not None
                if (not d_async) and d.eng == op.eng and op.dma_key is None and not raw and op.eng != "pool":
                    return
                s = d.stream()
                cur = deps.get(s)
                if cur is None or cur.pos < d.pos:
                    deps[s] = d
            for r in op.reads:
                w = last_w.get(r)
                if w is not None:
                    consider(w, True)
            for r in op.writes:
                w = last_w.get(r)
                if w is not None:
                    consider(w, False)
                for rd in readers.get(r, ()):
                    consider(rd, False)
            for r in op.reads + op.writes:
                for t in _tags(r):
                    for d in reg_fence.get(t, ()):
                        consider(d, True)
                    lst = reg_acc.setdefault(t, [])
                    if not lst or lst[-1] is not op:
                        lst.append(op)
            for s, d in deps.items():
                d.marked = True
                op.waits[s] = d
            for r in op.reads:
                readers.setdefault(r, []).append(op)
            for r in op.writes:
                last_w[r] = op
                readers[r] = []
            for r in op.reads:
                lst = readers[r]
                if len(lst) > 12:
                    best = {}
                    for o in lst:
                        s = o.stream()
                        if s not in best or best[s].pos < o.pos:
                            best[s] = o
                    readers[r] = sorted(best.values(), key=lambda o: o.pos)
        cnt = {}
        for op in self.ops:
            if op.fence is not None:
                continue
            if op.dma_key is not None:
                key = op.stream()
                cnt[key] = cnt.get(key, 0) + 16
                op.sig = (key, cnt[key])
                op.marked = True
            elif op.marked:
                key = op.stream()
                cnt[key] = cnt.get(key, 0) + 1
                op.sig = (key, cnt[key])
        for op in self.ops:
            if op.dma_key is not None and op.dma_key.startswith("const"):
                op.sig = (op.sig[0], cnt[op.sig[0]])
        self.final = cnt

    def emit(self, nc, final_wait_keys=()):
        self.analyze()
        with ExitStack() as es:
            sems = {k: es.enter_context(nc.semaphore(k)) for k in sorted(self.final)}
            block = es.enter_context(nc.Block())

            def run(engname):
                def body(e):
                    waited = {}
                    for op in self.ops:
                        if op.eng != engname:
                            continue
                        for s, d in op.waits.items():
                            k, v = d.sig
                            if waited.get(k, 0) < v:
                                e.wait_ge(sems[k], v)
                                waited[k] = v
                        ins = op.fn(e)
                        if op.marked:
                            k, v = op.sig
                            ins.then_inc(sems[k], 16 if op.dma_key is not None else 1)
                    if engname == "sp":
                        for k in final_wait_keys:
                            kk = "dma_" + k
                            if kk in self.final:
                                e.wait_ge(sems[kk], self.final[kk])
                return body

            block.tensor(run("pe"))
            block.scalar(run("act"))
            block.vector(run("dve"))
            block.gpsimd(run("pool"))
            block.sync(run("sp"))


class Ring:
    def __init__(self, S, name, nslots, queue, tag=None):
        self.S = S
        self.name = name
        self.n = nslots
        self.queue = queue
        self.tag = tag
        self.start = S.cur()
        self.reqs = []

    def res(self, slot, k):
        if self.tag:
            return (self.tag, self.name, slot, k)
        return (self.name, slot, k)

    def request(self, dma_fns):
        j = len(self.reqs)
        slot = j % self.n
        self.reqs.append([dma_fns, None, None, self.S.cur()])
        return j, slot, [self.res(slot, k) for k in range(len(dma_fns))]

    def used(self, j):
        self.reqs[j][1] = self.S.cur()
        if self.reqs[j][2] is None:
            self.reqs[j][2] = self.S.cur()

    def finalize(self):
        for j, (fns, _, first, reqpos) in enumerate(self.reqs):
            slot = j % self.n
            if j < self.n:
                base = self.start + 1e-4 * (j + 1)
            else:
                lp = self.reqs[j - self.n][1]
                assert lp is not None, (self.name, j)
                base = lp + 0.3 + 1e-4 * j
            assert first is not None and base < first, (self.name, j, base, first)
            for k, fn in enumerate(fns):
                self.S.add(self.queue, (lambda fn, slot: lambda e: fn(e, slot))(fn, slot),
                           reads=(), writes=[self.res(slot, k)],
                           dma_key=f"{self.name}{slot}_{k}", pos=base + 1e-6 * k)


def build(NL=DEPTH):
    nc = bass.Bass("TRN2", target_bir_lowering=False)

    def dram(n, s, dt=F32, kind="ExternalInput"):
        return nc.dram_tensor(n, list(s), dt, kind=kind).ap()

    x_d = dram("x", [T, DM])
    p_d = dram("p", [DEPTH, T, 256])
    w_in_d = dram("w_in", [DEPTH, DM, 8192])
    binT_d = dram("binT", [128, DEPTH, 64])
    bav_d = dram("bav", [DEPTH, 1, 512])
    alg_d = dram("alg", [DEPTH, 1, 512])
    alb_d = dram("alb", [DEPTH, 1, 512])
    aws_d = dram("aws", [DEPTH, 8, 128, 128])
    abs_d = dram("absr", [DEPTH, 1, 1024])
    cw_d = dram("convw", [128, DEPTH, 4, 31])
    cvec_d = dram("cvec", [128, DEPTH, 3, 4])
    wpa_d = dram("w_pa", [DEPTH, 512, DM])
    wpb_d = dram("w_pb", [DEPTH, 512, DM])
    wpc_d = dram("w_pc", [DEPTH, 512, DM])
    wout_d = dram("w_out", [DEPTH, DM, DM])
    wple_d = dram("w_ple", [DEPTH, 256, DM])
    ovec_d = dram("ovec", [128, DEPTH, 3, 8])
    ident_d = dram("ident", [128, 128])
    chand_d = dram("chand", [128, 257], BF16)
    dft_d = dram("dft", [2, T, T], BF16)
    out_d = dram("out", [T, DM], F32, kind="ExternalOutput")

    S = Sched()
    es = ExitStack()
    with es:
        def sb(n, s, dt=F32):
            return es.enter_context(nc.sbuf_tensor(n, list(s), dt))

        xT32 = sb("xT32", [128, NCH, T])
        xTb = sb("xTb", [128, NCH, T], BF16)
        wring = sb("wring", [128, RING, 8, 128], BF16)
        ident = sb("ident_s", [128, 128])
        identb = sb("identb", [128, 128], BF16)
        onesb = sb("onesb", [128, 128], BF16)
        bdiag = sb("bdiag", [128, 128], BF16)
        chand = sb("chand_s", [128, 258], BF16)
        neghalf = sb("neghalf", [128, 2])
        binT = sb("binT_s", [128, DEPTH, 64])
        cvec = sb("cvec_s", [128, DEPTH, 3, 4])
        ovec = sb("ovec_s", [128, DEPTH, 3, 8])
        convw = sb("convw_s", [128, DEPTH, 4, 31])
        g8 = sb("g8", [128, DEPTH, 4])
        epsT = sb("epsT", [128, 1])
        AR_BYTES = 92 * 1024
        arena = sb("arena", [128, AR_BYTES // 4])
        psum = [es.enter_context(nc.psum_tensor(f"ps{i}", [128, 512], F32)) for i in range(8)]

        Y0, Y1, Y2, M_, TM = 0, 16384, 32768, 49152, 81920

        def av(off, shape, dt=F32):
            esz = 4 if dt == F32 else 2
            nel = int(np.prod(shape[1:]))
            nbytes = nel * esz
            assert off % 4 == 0 and nbytes % 4 == 0 and off + nbytes <= AR_BYTES, (off, shape)
            v = arena[:, off // 4:(off + nbytes) // 4]
            if dt != F32:
                v = v.bitcast(dt)
            if len(shape) == 3:
                v = v.rearrange("p (a b) -> p a b", a=shape[1])
            elif len(shape) == 4:
                v = v.rearrange("p (a b c) -> p a b c", a=shape[1], b=shape[2])
            return v

        class PS:
            reserved = set()
            i = 0

            @classmethod
            def next(cls):
                while True:
                    b = cls.i % 8
                    cls.i += 1
                    if b not in cls.reserved:
                        return b

        def psr(b):
            return [("ps", b), ("pp", b)]

        def mm(bank, out_ap, pairs, reads, start=True, stop=True, extra_w=()):
            pairs = list(pairs)

            def fn(e):
                ins = None
                n = len(pairs)
                for i, (l, r) in enumerate(pairs):
                    ins = e.matmul(out_ap, l, r, start=(start and i == 0), stop=(stop and i == n - 1))
                return ins
            return S.add("pe", fn, reads, psr(bank) + list(extra_w))

        def mm_multi(bank, groups, reads):
            groups = [(o, list(pr)) for o, pr in groups]

            def fn(e):
                ins = None
                for o, pr in groups:
                    n = len(pr)
                    for i, (l, r) in enumerate(pr):
                        ins = e.matmul(o, l, r, start=(i == 0), stop=(i == n - 1))
                return ins
            return S.add("pe", fn, reads, psr(bank))

        def tr_multi(bank, items, reads):
            items = list(items)

            def fn(e):
                ins = None
                for o, i_ in items:
                    ins = e.transpose(o, i_, ident[:])
                return ins
            return S.add("pe", fn, list(reads) + ["ident"], psr(bank))

        def act(out, in_, func, reads, writes, bias=0.0, scale=1.0, pbank=None):
            w = list(writes) + ([("pp", pbank)] if pbank is not None else [])
            r = list(reads) + ([("ps", pbank)] if pbank is not None else [])
            return S.add("act", lambda e: e.activation(out=out, in_=in_, func=func, bias=bias, scale=scale), r, w)

        def vec(eng, kind, reads, writes, pbank=None, **kw):
            w = list(writes) + ([("pp", pbank)] if pbank is not None else [])
            r = list(reads) + ([("ps", pbank)] if pbank is not None else [])
            if kind == "tt":
                fn = lambda e: e.tensor_tensor(out=kw["out"], in0=kw["in0"], in1=kw["in1"], op=kw["op"])
            elif kind == "stt":
                fn = lambda e: e.scalar_tensor_tensor(out=kw["out"], in0=kw["in0"], scalar=kw["scalar"],
                                                      in1=kw["in1"], op0=kw["op0"], op1=kw["op1"])
            elif kind == "ts":
                fn = lambda e: e.tensor_scalar(out=kw["out"], in0=kw["in0"], scalar1=kw["s1"],
                                               scalar2=kw.get("s2"), op0=kw["op0"],
                                               **({"op1": kw["op1"]} if "op1" in kw else {}))
            elif kind == "copy":
                fn = lambda e: e.tensor_copy(kw["out"], kw["in_"])
            elif kind == "memset":
                fn = lambda e: e.memset(kw["out"], kw["val"])
            elif kind == "bn_stats":
                fn = lambda e: e.bn_stats(kw["out"], kw["in_"])
            elif kind == "bn_aggr":
                fn = lambda e: e.bn_aggr(kw["out"], kw["in_"])
            else:
                raise ValueError(kind)
            return S.add(eng, fn, r, w)

        def dma(queue, out, in_, reads, writes, key):
            return S.add(queue, lambda e: e.dma_start(out=out, in_=in_), reads, writes, dma_key=key)


        def wreq(src3, nch):
            def fn(e, slot):
                return e.dma_start(out=wring[:, slot, 0:nch, :], in_=src3)
            j, slot, res = WR.request([fn])
            return j, wring[:, slot, 0:nch, :], res

        def w_in_blk(l, j):
            return w_in_d[l].rearrange("(c p) n -> p c n", p=128)[:, :, j * 128:(j + 1) * 128]

        def w_blk(wd, l, j):
            return wd[l].rearrange("(c p) n -> p c n", p=128)[:, :, j * 128:(j + 1) * 128]

        def xres(tb):
            return [("xTb", c, tb) for c in range(NCH)]

        def tbs(tb):
            return slice(tb * 512, (tb + 1) * 512)

        dma("sp", ident[:], ident_d, [], ["ident"], "const")
        dma("sp", chand[:, 0:257], chand_d, [], ["chand"], "const")
        dma("sp", binT[:], binT_d, [], ["binT"], "const")
        dma("sp", cvec[:], cvec_d, [], ["cvec"], "const")
        dma("sp", ovec[:], ovec_d, [], ["ovec"], "const")
        dma("sp", convw[:], cw_d, [], ["convw"], "const")
        vec("pool", "memset", [], ["onesb"], out=onesb[:], val=1.0)
        vec("pool", "memset", [], ["neghalf"], out=neghalf[:], val=-0.5)
        vec("pool", "memset", [], ["epsT"], out=epsT[:], val=EPS)
        vec("pool", "memset", [], ["bdiag"], out=bdiag[:], val=0.0)
        vec("pool", "memset", [], ["bdiag"], out=bdiag[0:64, 0:64], val=1.0)
        vec("pool", "memset", [], ["bdiag"], out=bdiag[64:128, 64:128], val=1.0)
        vec("dve", "copy", ["ident"], ["identb"], out=identb[:], in_=ident[:])
        vec("dve", "ts", ["cvec"], ["g8"], out=g8[:], in0=cvec[:, :, 1, :], s1=1.0, op0=ALU.mult)

        xin = [av(M_ + s * 4096, [128, DM]) for s in range(4)]
        for t in range(NTT):
            s = t % 4
            dma("sp", xin[s], x_d[t * 128:(t + 1) * 128, :], [], [("@M", "xin", s)] + ([("xgate",)] if t == 11 else []),
                f"xin{s}")
            for half in range(2):
                b = PS.next()
                tr_multi(b, [(psum[b][:, k * 128:(k + 1) * 128],
                              xin[s][:, (half * 4 + k) * 128:(half * 4 + k + 1) * 128]) for k in range(4)],
                         [("@M", "xin", s)])
                src = psum[b][:].rearrange("p (a b) -> p a b", a=4)
                cs = slice(half * 4, half * 4 + 4)
                act(xT32[:, cs, t * 128:(t + 1) * 128], src, AF.Copy, [],
                    [("x32", c, t // 4) for c in range(half * 4, half * 4 + 4)], pbank=b)
                vec("dve", "copy", [], [("xTb", c, t // 4) for c in range(half * 4, half * 4 + 4)],
                    pbank=b, out=xTb[:, cs, t * 128:(t + 1) * 128], in_=src)

        WR = Ring(S, "wr", RING, "pool")
        rings = [WR]

        st1a = {}
        zT_all = av(Y0, [128, 4, T], BF16)

        def emit_1a_tb(l, tb):
            if l not in st1a:
                st1a[l] = [wreq(w_in_blk(l, 12 + g), 8) for g in range(4)]
            for g in range(4):
                j, W, wres = st1a[l][g]
                b = PS.next()
                mm(b, psum[b][:], [(W[:, c, :], xTb[:, c, tbs(tb)]) for c in range(NCH)],
                   wres + xres(tb))
                WR.used(j)
                act(zT_all[:, g, tbs(tb)], psum[b][:], AF.Identity, ["binT"], [("@Y0", "z", g, tb)],
                    bias=binT[:, l, 12 + g:13 + g], pbank=b)

        pre1a = set()

        def layer(l, last):
            def bias(j):
                return binT[:, l, j:j + 1]

            S.fence(["Y0", "Y1", "Y2", "M", "TM"])
            zT = av(Y0, [128, 4, T], BF16)
            U = av(M_, [128, NTT, 1024], BF16)
            stage = [av(Y1 + s * 8192, [128, 2, 16, 128], BF16) for s in range(2)]
            wbg = av(Y2, [128, 4, 8, 128], BF16)
            ST = Ring(S, f"stg{l}_", 2, "sp", tag="@Y1")
            rings.append(ST)

            algt = av(TM, [128, 512])
            albt = av(TM + 2048, [128, 512])
            wsT = av(TM + 4096, [128, 8, 128], BF16)
            e2 = av(TM + 6144, [128, 4, 128])
            bavr = av(TM + 8192, [128, 512], BF16)
            onesr = av(TM + 9216, [128, 128], BF16)

            dma("sp", algt, alg_d[l].to_broadcast([128, 512]), [], [("@TM", "alg")], "p2a")
            dma("sp", albt, alb_d[l].to_broadcast([128, 512]), [], [("@TM", "alb")], "p2b")
            for h in range(8):
                dma("sp", e2[(h % 2) * 64:(h % 2) * 64 + 64, h // 2, :],
                    abs_d[l][0:1, h * 128:(h + 1) * 128].to_broadcast([64, 128]), [], [("@TM", "e2", h)], f"p2c{h}")
            dma("pool", bavr[0:1, :], bav_d[l], [], [("@TM", "bavr")], "p2d")
            vec("pool", "memset", [], [("@TM", "onesr")], out=onesr[0:1, :], val=1.0)
            wtmp = av(Y2 + 10240, [128, 8, 128])
            WT = [("@Y2", "wtmp")]
            dma("sp", wtmp, aws_d[l].rearrange("h q k -> q h k"), [], WT, "p2e")
            for half in range(2):
                b = PS.next()
                tr_multi(b, [(psum[b][:, k * 128:(k + 1) * 128], wtmp[:, half * 4 + k, :]) for k in range(4)],
                         WT)
                vec("dve", "copy", [], [("@TM", "wsT", half)], pbank=b, out=wsT[:, half * 4:half * 4 + 4, :],
                    in_=psum[b][:].rearrange("p (a b) -> p a b", a=4))

            for g in range(4):
                dma("pool", wbg[:, g], w_in_blk(l, 16 + g), [], [("@Y2", "wbg", g)], f"wbg{g}")
            if l not in pre1a:
                for tb in range(NTB):
                    emit_1a_tb(l, tb)
            k = 0
            for t in range(NTT):
                for gp in range(2):
                    b = PS.next()
                    mm_multi(b, [(psum[b][:, q * 256:(q + 1) * 256],
                                  [(zT[:, 2 * gp + q, t * 128:(t + 1) * 128], chand[:, 0:256])]) for q in range(2)],
                             [("@Y0", "z", 2 * gp, t // 4), ("@Y0", "z", 2 * gp + 1, t // 4), "chand"])
                    if k % 2 == 0:
                        act(U[:, t, gp * 512:(gp + 1) * 512], psum[b][:], AF.Copy, [],
                            [("@M", "U", t, gp)], pbank=b)
                    else:
                        vec("dve", "copy", [], [("@M", "U", t, gp)], pbank=b,
                            out=U[:, t, gp * 512:(gp + 1) * 512], in_=psum[b][:])
                    k += 1
            ybT = zT
            t1d = [[av(Y2 + (8192 if i < 2 else 14336) + (s * 2 + (i % 2)) * 512, [128, 128]) for i in range(4)]
                   for s in range(2)]

            def tbl(lo, hi):
                return sorted(set(range(lo // 512, (hi - 1) // 512 + 1)))
            it1d = 0
            for qb in range(8):
                q0 = qb * 128
                m0 = T - q0 - 127
                nm = 128 if qb > 0 else 127

                def mk(cs, q0=q0):
                    src = dft_d[cs].rearrange("(t p) q -> p t q", p=128)[:, :, q0:q0 + 128]
                    return lambda e, slot: e.dma_start(out=stage[slot][:, cs], in_=src)
                js, slot, sres = ST.request([mk(0), mk(1)])
                for g in range(4):
                    s2 = it1d % 2
                    it1d += 1
                    sg1, sg2, pim, t2 = t1d[s2]
                    RT = [("@Y2", "t1d", s2, i) for i in range(4)]
                    ba = PS.next()
                    mm_multi(ba, [(psum[ba][:, 0:128], [(wbg[:, g, c, :], xTb[:, c, q0:q0 + 128]) for c in range(NCH)]),
                                  (psum[ba][:, 128:128 + nm], [(wbg[:, g, c, :], xTb[:, c, m0:m0 + nm]) for c in range(NCH)])],
                             [("@Y2", "wbg", g)] + xres(qb // 4) + [r for tb_ in tbl(m0, m0 + nm) for r in xres(tb_)])
                    bb = PS.next()
                    mm_multi(bb, [(psum[bb][:, 0:128], [(U[:, t, g * 256:g * 256 + 128], stage[slot][:, 0, t, :]) for t in range(NTT)]),
                                  (psum[bb][:, 128:256], [(U[:, t, g * 256 + 128:g * 256 + 256], stage[slot][:, 1, t, :]) for t in range(NTT)])],
                             sres + [("@M", "U", t, g // 2) for t in range(NTT)])
                    ST.used(js)
                    act(sg1, psum[ba][:, 0:128], AF.Silu, ["binT"], [RT[0]], bias=bias(16 + g), pbank=ba)
                    act(sg2[:, 0:nm], psum[ba][:, 128:128 + nm], AF.Silu, ["binT"], [RT[1]], bias=bias(16 + g), pbank=ba)
                    act(pim, psum[bb][:, 128:256], AF.Copy, [], [RT[2]], pbank=bb)
                    vec("dve", "tt", [RT[2]], [RT[3]], pbank=bb, out=t2, in0=psum[bb][:, 0:128], in1=pim, op=ALU.subtract)
                    vec("dve", "tt", [RT[2]], [RT[2]], pbank=bb, out=pim, in0=psum[bb][:, 0:128], in1=pim, op=ALU.add)
                    vec("dve", "tt", [RT[2], RT[0]], [("@Y0", "yb", g, qb // 4)],
                        out=ybT[:, g, q0:q0 + 128], in0=sg1, in1=pim, op=ALU.mult)
                    vec("dve", "tt", [RT[3], RT[1]], [("@Y0", "yb", g, tb_) for tb_ in tbl(m0, m0 + nm)],
                        out=ybT[:, g, m0:m0 + nm], in0=sg2[:, 0:nm], in1=t2[:, 127:127 - nm:-1] if nm < 128 else t2[:, ::-1],
                        op=ALU.mult)
            bn = PS.next()
            mm_multi(bn, [(psum[bn][:, g:g + 1], [(U[:, t, g * 256:g * 256 + 128], chand[:, 256:257]) for t in range(NTT)])
                          for g in range(4)] +
                         [(psum[bn][:, 8 + g:9 + g], [(wbg[:, g, c, :], xTb[:, c, 1024:1025]) for c in range(NCH)])
                          for g in range(4)],
                     ["chand"] + [("@M", "U", t, gp) for t in range(NTT) for gp in range(2)] +
                     [("@Y2", "wbg", g) for g in range(4)] + xres(2))
            nyq = t1d[0][0]
            RN = ("@Y2", "t1d", 0, 0)
            for g in range(4):
                act(nyq[:, g:g + 1], psum[bn][:, 8 + g:9 + g], AF.Silu, ["binT"], [RN], bias=bias(16 + g), pbank=bn)
            vec("dve", "tt", [RN], [RN], pbank=bn, out=nyq[:, 4:8], in0=nyq[:, 0:4], in1=psum[bn][:, 0:4], op=ALU.mult)
            for g in range(4):
                vec("dve", "copy", [RN], [("@Y0", "yb", g, 2)], out=ybT[:, g, 1024:1025], in_=nyq[:, 4 + g:5 + g])

            S.fence(["Y1", "Y2", "M"])
            ycT = av(Y1, [128, 4, T], BF16)
            P3 = Y2
            hT = [av(P3 + s * 4160, [128, 2080], BF16) for s in range(2)]
            Dg = [av(P3 + 8320 + s * 7936, [128, 31, 128], BF16) for s in range(2)]
            o = P3 + 8320 + 2 * 7936
            s_t = [av(o + s * 2048, [128, 512]) for s in range(2)]; o += 4096
            c32 = [av(o + s * 2048, [128, 512]) for s in range(3)]; o += 6144
            c16 = [av(o + s * 1024, [128, 512], BF16) for s in range(3)]; o += 3072
            vE = [av(o + s * 2048, [128, 512]) for s in range(2)]; o += 4096
            sgc = [av(o + s * 2048, [128, 512]) for s in range(3)]; o += 6144
            R3 = "@Y2+M"
            for s in range(2):
                vec("pool", "memset", [], [(R3, "h", s, k) for k in range(4)], out=hT[s][:], val=0.0)

            tiles = []

            def buildD(cb):
                for j in range(31):
                    vec("dve", "ts", ["identb", "convw"], [(R3, "D", cb % 2)], out=Dg[cb % 2][:, j, :], in0=identb[:],
                        s1=convw[:, l, cb, j:j + 1], op0=ALU.mult)

            def stage0(cb):
                hs = cb % 2
                jg, Wg, rg = wreq(w_in_blk(l, 24 + cb), 8)
                jv, Wv, rv = wreq(w_in_blk(l, 20 + cb), 8)
                for tb in range(NTB):
                    b1 = PS.next()
                    mm(b1, psum[b1][:], [(Wg[:, c, :], xTb[:, c, tbs(tb)]) for c in range(NCH)], rg + xres(tb))
                    WR.used(jg)
                    b2 = PS.next()
                    mm(b2, psum[b2][:], [(Wv[:, c, :], xTb[:, c, tbs(tb)]) for c in range(NCH)], rv + xres(tb))
                    WR.used(jv)
                    s2 = tb % 2
                    act(s_t[s2], psum[b1][:], AF.Sigmoid, ["binT"], [(R3, "s", s2)], bias=bias(24 + cb), pbank=b1)
                    vec("dve", "stt", [(R3, "s", s2), "binT"], [(R3, "h", hs, tb)], pbank=b2,
                        out=hT[hs][:, 15 + tb * 512:15 + (tb + 1) * 512], in0=psum[b2][:],
                        scalar=bias(20 + cb), in1=s_t[s2], op0=ALU.add, op1=ALU.mult)

            def stageA(i, cb, tb, jc, Wc, rc):
                hs = cb % 2
                k3 = i % 3
                b = PS.next()
                hres = [(R3, "h", hs, t2) for t2 in range(max(0, tb - 1), min(NTB, tb + 2))]
                mm(b, psum[b][:], [(Dg[cb % 2][:, j, :], hT[hs][:, tb * 512 + j:tb * 512 + j + 512]) for j in range(31)],
                   [(R3, "D", cb % 2)] + hres)
                act(c32[k3], psum[b][:], AF.Identity, ["cvec"], [(R3, "c32", k3)], bias=cvec[:, l, 0, cb:cb + 1], pbank=b)
                vec("pool", "copy", [(R3, "c32", k3)], [(R3, "c16", k3)], out=c16[k3], in_=c32[k3])
                b2 = PS.next()
                mm(b2, psum[b2][:], [(Wc[:, c, :], xTb[:, c, tbs(tb)]) for c in range(NCH)], rc + xres(tb))
                WR.used(jc)
                act(sgc[k3], psum[b2][:], AF.Silu, ["binT"], [(R3, "sgc", k3)], bias=bias(28 + cb), pbank=b2)

            def stageB(i, cb, tb):
                k3 = i % 3
                b = PS.next()
                mm(b, psum[b][:], [(bdiag[:], c16[k3])], ["bdiag", (R3, "c16", k3)])
                vec("dve", "stt", [(R3, "c32", k3)], [(R3, "c32", k3)], pbank=b, out=c32[k3], in0=psum[b][:],
                    scalar=-1.0 / 64, in1=c32[k3], op0=ALU.mult, op1=ALU.add)
                act(c16[k3], c32[k3], AF.Square, [(R3, "c32", k3)], [(R3, "c16", k3)])

            def stageC(i, cb, tb):
                k3 = i % 3
                k2 = i % 2
                b = PS.next()
                mm(b, psum[b][:], [(bdiag[:], c16[k3])], ["bdiag", (R3, "c16", k3)])
                act(vE[k2], psum[b][:], AF.Sqrt, ["epsT"], [(R3, "vE", k2)], bias=epsT[:, 0:1], scale=1.0 / 64, pbank=b)
                S.add("dve", (lambda o_: lambda e: e.reciprocal(o_, o_))(vE[k2]), [(R3, "vE", k2)], [(R3, "vE", k2)])
                vec("dve", "tt", [(R3, "vE", k2), (R3, "c32", k3)], [(R3, "c32", k3)], out=c32[k3], in0=c32[k3],
                    in1=vE[k2], op=ALU.mult)
                act(c32[k3], c32[k3], AF.Silu, [(R3, "c32", k3), "g8", "cvec"], [(R3, "c32", k3)],
                    bias=cvec[:, l, 2, cb:cb + 1], scale=g8[:, l, cb:cb + 1])
                vec("dve", "tt", [(R3, "c32", k3), (R3, "sgc", k3)], [("@Y1", "yc", cb, tb)],
                    out=ycT[:, cb, tbs(tb)], in0=c32[k3], in1=sgc[k3], op=ALU.mult)

            i = 0
            pend = []
            buildD(0)
            for cb in range(4):
                stage0(cb)
                if cb + 1 < 4:
                    buildD(cb + 1)
                jc, Wc, rc = wreq(w_in_blk(l, 28 + cb), 8)
                for tb in range(NTB):
                    stageA(i, cb, tb, jc, Wc, rc)
                    pend.append((i, cb, tb))
                    if len(pend) >= 2:
                        stageB(*pend[-2])
                    if len(pend) >= 3:
                        stageC(*pend[-3])
                    i += 1
            stageB(*pend[-1])
            stageC(*pend[-2])
            stageC(*pend[-1])

            S.fence(["Y2", "M"])
            yaT = av(Y2, [128, 4, T], BF16)
            vn = av(M_, [128, NTT, 512], BF16)
            o = M_ + 16384
            gv = [av(o + s * 2048, [128, 512]) for s in range(2)]; o += 4096
            ut = [av(o + s * 2048, [128, 512]) for s in range(2)]; o += 4096
            sga = [av(o + s * 2048, [128, 512]) for s in range(2)]; o += 4096
            st6 = [av(o + s * 32, [128, 6]) for s in range(2)]; o += 64
            mvv = [av(o + s * 8, [128, 2]) for s in range(2)]; o += 16
            rsv = [av(o + s * 8, [128, 2]) for s in range(2)]; o += 16
            wav = [wreq(w_in_blk(l, 4 + j), 8) for j in range(4)]
            for t in range(NTT):
                b = PS.next()
                groups = []
                rd = [("@TM", "bavr"), ("@TM", "onesr")] + xres(t // 4)
                for j in range(4):
                    jj, W, wres = wav[j]
                    pr = [(xTb[:, c, t * 128:(t + 1) * 128], W[:, c, :]) for c in range(NCH)]
                    pr.append((onesr[0:1, :], bavr[0:1, j * 128:(j + 1) * 128]))
                    groups.append((psum[b][:, j * 128:(j + 1) * 128], pr))
                    rd += wres
                mm_multi(b, groups, rd)
                for j in range(4):
                    WR.used(wav[j][0])
                s2 = t % 2
                act(gv[s2], psum[b][:], AF.Gelu_apprx_tanh, [], [("@M", "gv", s2)], pbank=b)
                vec("dve", "bn_stats", [("@M", "gv", s2)], [("@M", "st6", s2)], out=st6[s2], in_=gv[s2])
                vec("dve", "bn_aggr", [("@M", "st6", s2)], [("@M", "mv", s2)], out=mvv[s2], in_=st6[s2])
                vec("dve", "ts", [("@M", "mv", s2)], [("@M", "rs", s2)], out=rsv[s2][:, 0:1], in0=mvv[s2][:, 1:2],
                    s1=EPS, op0=ALU.add)
                vec("pool", "tt", [("@M", "rs", s2), "neghalf"], [("@M", "rs", s2)], out=rsv[s2][:, 0:1],
                    in0=rsv[s2][:, 0:1], in1=neghalf[:, 0:1], op=ALU.pow)
                vec("dve", "stt", [("@M", "gv", s2), ("@M", "mv", s2), ("@TM", "alg")], [("@M", "gv", s2)],
                    out=gv[s2], in0=gv[s2], scalar=mvv[s2][:, 0:1], in1=algt, op0=ALU.subtract, op1=ALU.mult)
                vec("dve", "stt", [("@M", "gv", s2), ("@M", "rs", s2), ("@TM", "alb")], [("@M", "vn", t)],
                    out=vn[:, t, :], in0=gv[s2], scalar=rsv[s2][:, 0:1], in1=albt, op0=ALU.mult, op1=ALU.add)

            it = 0
            for cb in range(4):
                ju, Wu, ru = wreq(w_in_blk(l, cb), 8)
                jg, Wg, rg = wreq(w_in_blk(l, 8 + cb), 8)
                for tp in range(2):
                    bu, bg, bm = [], [], []
                    for q in range(2):
                        tb = tp * 2 + q
                        b = PS.next(); bu.append(b)
                        mm(b, psum[b][:], [(Wu[:, c, :], xTb[:, c, tbs(tb)]) for c in range(NCH)], ru + xres(tb))
                        WR.used(ju)
                    for q in range(2):
                        tb = tp * 2 + q
                        b = PS.next(); bg.append(b)
                        mm(b, psum[b][:], [(Wg[:, c, :], xTb[:, c, tbs(tb)]) for c in range(NCH)], rg + xres(tb))
                        WR.used(jg)
                    for q in range(2):
                        tb = tp * 2 + q
                        b = PS.next(); bm.append(b)
                        groups = []
                        for n4 in range(4):
                            n = tb * 4 + n4
                            for hh in range(2):
                                h = 2 * cb + hh
                                groups.append((psum[b][hh * 64:(hh + 1) * 64, n4 * 128:(n4 + 1) * 128],
                                               [(vn[:, n, h * 64:(h + 1) * 64], wsT[:, h, :])]))
                        mm_multi(b, groups, [("@M", "vn", tb * 4 + n4) for n4 in range(4)] +
                                 [("@TM", "wsT", cb // 2)])
                    for q in range(2):
                        s2 = q
                        act(ut[s2], psum[bu[q]][:], AF.Gelu_apprx_tanh, ["binT"], [("@M", "ut", s2)],
                            bias=bias(cb), pbank=bu[q])
                    for q in range(2):
                        s2 = q
                        act(sga[s2], psum[bg[q]][:], AF.Silu, ["binT"], [("@M", "sga", s2)],
                            bias=bias(8 + cb), pbank=bg[q])
                    for q in range(2):
                        tb = tp * 2 + q
                        s2 = q
                        vec("dve", "tt", [("@M", "ut", s2), ("@M", "sga", s2)], [("@M", "ut", s2)],
                            out=ut[s2], in0=ut[s2], in1=sga[s2], op=ALU.mult)
                        vec("dve", "tt", [("@TM", "e2", 2 * cb), ("@TM", "e2", 2 * cb + 1)], [("@M", "sga", s2)],
                            pbank=bm[q], out=sga[s2].rearrange("p (a b) -> p a b", a=4),
                            in0=psum[bm[q]][:].rearrange("p (a b) -> p a b", a=4),
                            in1=e2[:, cb:cb + 1, :].to_broadcast([128, 4, 128]), op=ALU.add)
                        vec("dve", "tt", [("@M", "ut", s2), ("@M", "sga", s2)], [("@Y2", "ya", cb, tb)],
                            out=yaT[:, cb, tbs(tb)], in0=ut[s2], in1=sga[s2], op=ALU.mult)

            S.fence(["M", "TM"])
            mT = av(M_, [128, NCH, T], BF16)
            macc = [av(TM + tb * 2048, [128, 512]) for tb in range(NTB)]
            gt = [av(TM + 8192 + s * 2048, [128, 512]) for s in range(2)]
            ys = [yaT, ybT, ycT]
            ytag = [("@Y2", "ya"), ("@Y0", "yb"), ("@Y1", "yc")]
            wds = (wpa_d, wpb_d, wpc_d)
            it = 0
            for fb in range(8):
                for i in range(3):
                    jg, Wg, rg = wreq(w_in_blk(l, 32 + i * 8 + fb), 8)
                    jp, Wp, rp = wreq(w_blk(wds[i], l, fb), 4)
                    for tb in range(NTB):
                        s2 = it % 2
                        it += 1
                        b1 = PS.next()
                        mm(b1, psum[b1][:], [(Wg[:, c, :], xTb[:, c, tbs(tb)]) for c in range(NCH)], rg + xres(tb))
                        WR.used(jg)
                        b2 = PS.next()
                        mm(b2, psum[b2][:], [(Wp[:, c, :], ys[i][:, c, tbs(tb)]) for c in range(4)],
                           rp + [ytag[i] + (c, tb) for c in range(4)])
                        WR.used(jp)
                        act(gt[s2], psum[b1][:], AF.Sigmoid, ["binT"], [("@TM", "gt", s2)],
                            bias=bias(32 + i * 8 + fb), pbank=b1)
                        if i == 0:
                            vec("dve", "tt", [("@TM", "gt", s2)], [("@TM", "macc", tb)], pbank=b2,
                                out=macc[tb], in0=gt[s2], in1=psum[b2][:], op=ALU.mult)
                        else:
                            vec("dve", "tt", [("@TM", "gt", s2)], [("@TM", "gt", s2)], pbank=b2,
                                out=gt[s2], in0=gt[s2], in1=psum[b2][:], op=ALU.mult)
                            if i == 1:
                                vec("pool", "tt", [("@TM", "gt", s2), ("@TM", "macc", tb)], [("@TM", "macc", tb)],
                                    out=macc[tb], in0=macc[tb], in1=gt[s2], op=ALU.add)
                            else:
                                vec("pool", "tt", [("@TM", "gt", s2), ("@TM", "macc", tb)], [("@M", "mT", fb, tb)],
                                    out=mT[:, fb, tbs(tb)], in0=macc[tb], in1=gt[s2], op=ALU.add)

            S.fence(["Y0", "Y1", "Y2", "TM"])
            vbf = av(Y0, [128, NCH, T], BF16)
            VB = "@Y0+Y1"
            pT = av(Y2, [128, 2, T], BF16)
            sg5 = [av(TM + s * 2048, [128, 512]) for s in range(2)]
            mx5 = [av(TM + 4096 + s * 2048, [128, 512]) for s in range(2)]
            pst = av(Y2 + 8192, [128, 8, 256])
            for hh in range(2):
                dma("sp", pst, p_d[l, hh * 1024:(hh + 1) * 1024, :].rearrange("(t p) f -> p t f", p=128), [],
                    [("@Y2", "pst")], "pst")
                for t8 in range(8):
                    t = hh * 8 + t8
                    b = PS.next()
                    tr_multi(b, [(psum[b][:, k * 128:(k + 1) * 128], pst[:, t8, k * 128:(k + 1) * 128]) for k in range(2)],
                             [("@Y2", "pst")])
                    vec("dve", "copy", [], [("@Y2", "pT", t // 4)], pbank=b, out=pT[:, :, t * 128:(t + 1) * 128],
                        in_=psum[b][:, 0:256].rearrange("p (a b) -> p a b", a=2))
            it = 0
            for ob in range(8):
                jo, Wo, ro = wreq(w_blk(wout_d, l, ob), 8)
                jq, Wq, rq = wreq(w_in_blk(l, 56 + ob), 8)
                je, We, re_ = wreq(w_blk(wple_d, l, ob), 2)
                for tb in range(NTB):
                    s2 = it % 2
                    it += 1
                    b1 = PS.next()
                    mm(b1, psum[b1][:], [(Wo[:, c, :], mT[:, c, tbs(tb)]) for c in range(NCH)],
                       ro + [("@M", "mT", c, tb) for c in range(NCH)])
                    WR.used(jo)
                    b2 = PS.next()
                    mm(b2, psum[b2][:], [(Wq[:, c, :], xTb[:, c, tbs(tb)]) for c in range(NCH)], rq + xres(tb))
                    WR.used(jq)
                    b3 = PS.next()
                    mm(b3, psum[b3][:], [(We[:, c, :], pT[:, c, tbs(tb)]) for c in range(2)],
                       re_ + [("@Y2", "pT", tb)])
                    WR.used(je)
                    act(sg5[s2], psum[b2][:], AF.Sigmoid, ["binT"], [("@TM", "sg5", s2)], bias=bias(56 + ob), pbank=b2)
                    act(mx5[s2], psum[b1][:], AF.Identity, ["ovec"], [("@TM", "mx5", s2)],
                        bias=ovec[:, l, 0, ob:ob + 1], pbank=b1)
                    vec("dve", "stt", [("x32", ob, tb), ("@TM", "mx5", s2)], [("@TM", "mx5", s2)], out=mx5[s2],
                        in0=xT32[:, ob, tbs(tb)], scalar=ALPHA, in1=mx5[s2], op0=ALU.mult, op1=ALU.add)
                    vec("dve", "tt", [("@TM", "sg5", s2)], [("@TM", "sg5", s2)], pbank=b3, out=sg5[s2],
                        in0=sg5[s2], in1=psum[b3][:], op=ALU.mult)
                    vec("dve", "tt", [("@TM", "sg5", s2), ("@TM", "mx5", s2)], [("x32", ob, tb)],
                        out=xT32[:, ob, tbs(tb)], in0=mx5[s2], in1=sg5[s2], op=ALU.add)
                    vec("pool", "copy", [("x32", ob, tb)], [(VB, "vb", ob, tb)], out=vbf[:, ob, tbs(tb)],
                        in_=xT32[:, ob, tbs(tb)])

            S.fence(["TM", "Y2"])
            sq6 = [av(Y2 + 10240 + s * 1024, [128, 512], BF16) for s in range(2)]
            vE6 = [av(Y2 + 12288 + s * 2048, [128, 512]) for s in range(2)]
            ost = [av(TM + s * 4096, [128, DM]) for s in range(2)]
            cnt6 = [0]

            def stageX(tb):
                mb = PS.next()
                mm(mb, psum[mb][:], [(onesb[:], vbf[:, c, tbs(tb)]) for c in range(NCH)],
                   ["onesb"] + [(VB, "vb", c, tb) for c in range(NCH)])
                bv = PS.next()
                for c in range(NCH):
                    s2 = cnt6[0] % 2
                    cnt6[0] += 1
                    vec("dve", "stt", [("x32", c, tb)], [("x32", c, tb)], pbank=mb, out=xT32[:, c, tbs(tb)],
                        in0=psum[mb][:], scalar=-1.0 / DM, in1=xT32[:, c, tbs(tb)], op0=ALU.mult, op1=ALU.add)
                    act(sq6[s2], xT32[:, c, tbs(tb)], AF.Square, [("x32", c, tb)], [("@Y2", "sq6", s2)])
                    mm(bv, psum[bv][:], [(onesb[:], sq6[s2])], ["onesb", ("@Y2", "sq6", s2)],
                       start=(c == 0), stop=(c == NCH - 1))
                k2 = tb % 2
                act(vE6[k2], psum[bv][:], AF.Sqrt, ["epsT"], [("@Y2", "vE6", k2)], bias=epsT[:, 0:1],
                    scale=1.0 / DM, pbank=bv)
                S.add("dve", (lambda o_: lambda e: e.reciprocal(o_, o_))(vE6[k2]), [("@Y2", "vE6", k2)],
                      [("@Y2", "vE6", k2)])

            def stageY(tb):
                k2 = tb % 2
                for c in range(NCH):
                    vec("dve", "tt", [("x32", c, tb), ("@Y2", "vE6", k2)], [("x32", c, tb)],
                        out=xT32[:, c, tbs(tb)], in0=xT32[:, c, tbs(tb)], in1=vE6[k2], op=ALU.mult)
                    act(xT32[:, c, tbs(tb)], xT32[:, c, tbs(tb)], AF.Identity, [("x32", c, tb), "ovec"],
                        [("x32", c, tb)], bias=ovec[:, l, 2, c:c + 1], scale=ovec[:, l, 1, c:c + 1])
                    if not last:
                        vec("dve", "copy", [("x32", c, tb)], [("xTb", c, tb)], out=xTb[:, c, tbs(tb)],
                            in_=xT32[:, c, tbs(tb)])
                if last:
                    for t4 in range(4):
                        t = tb * 4 + t4
                        s2 = t % 2
                        for half in range(2):
                            b = PS.next()
                            tr_multi(b, [(psum[b][:, k * 128:(k + 1) * 128],
                                          xT32[:, half * 4 + k, t * 128:(t + 1) * 128]) for k in range(4)],
                                     [("x32", half * 4 + k, tb) for k in range(4)])
                            if half == 0:
                                act(ost[s2][:, 0:512], psum[b][:], AF.Copy, [], [("@TM", "ost", s2, 0)], pbank=b)
                            else:
                                vec("dve", "copy", [], [("@TM", "ost", s2, 1)], pbank=b, out=ost[s2][:, 512:1024],
                                    in_=psum[b][:])
                        dma("sp", out_d[t * 128:(t + 1) * 128, :], ost[s2],
                            [("@TM", "ost", s2, 0), ("@TM", "ost", s2, 1)], [], f"out{s2}")
                else:
                    emit_1a_tb(l + 1, tb)
                    pre1a.add(l + 1)

            stageX(0)
            for tb in range(NTB):
                if tb + 1 < NTB:
                    stageX(tb + 1)
                stageY(tb)

        for l in range(NL):
            layer(l, l == NL - 1)
        for rg_ in rings:
            rg_.finalize()
        S.emit(nc, final_wait_keys=["out0", "out1"])
    return nc


_CONST = {}


def _consts():
    if not _CONST:
        n = np.arange(T, dtype=np.float64)
        ang = 2.0 * np.pi * ((n[:, None] * n[None, :]) % T) / T
        dft = np.stack([np.cos(ang), np.sin(ang)]) / 512.0
        _CONST["dft"] = dft.astype(np.float32).astype(ml_dtypes.bfloat16)
        c = np.arange(128, dtype=np.float64)
        a2 = 2.0 * np.pi * ((c[:, None] * c[None, :]) % 128) / 128
        alt = (((-1.0) ** c) / 512.0)[:, None]
        _CONST["chand"] = np.concatenate([np.cos(a2), -np.sin(a2), alt], axis=1).astype(np.float32).astype(ml_dtypes.bfloat16)
        _CONST["ident"] = np.eye(128, dtype=np.float32)
    return _CONST


_NC_CACHE = {}


def _prep_shared(inp):
    f = lambda a: np.ascontiguousarray(np.asarray(a, dtype=np.float32))
    c = _consts()
    sh = {
        "w_in": f(inp["w_in"]),
        "binT": f(np.transpose(np.asarray(inp["b_in"]).reshape(DEPTH, 64, 128), (2, 0, 1))),
        "bav": f(np.asarray(inp["b_in"])[:, None, 512:1024]),
        "alg": f(np.asarray(inp["a_ln_g"])[:, None, :]),
        "alb": f(np.asarray(inp["a_ln_b"])[:, None, :]),
        "aws": f(inp["a_ws"]),
        "absr": f(np.asarray(inp["a_bs"]).reshape(DEPTH, 1, 1024)),
        "convw": f(np.transpose(np.asarray(inp["c_conv_w"]).reshape(DEPTH, 31, 4, 128), (3, 0, 2, 1))),
        "cvec": f(np.transpose(np.stack([np.asarray(inp[k]).reshape(DEPTH, 4, 128)
                                         for k in ("c_conv_b", "c_ln_g", "c_ln_b")], axis=1), (3, 0, 1, 2))),
        "w_pa": f(inp["w_pa"]), "w_pb": f(inp["w_pb"]), "w_pc": f(inp["w_pc"]),
        "w_out": f(inp["w_out"]), "w_ple": f(inp["w_ple"]),
        "ovec": f(np.transpose(np.stack([np.asarray(inp[k]).reshape(DEPTH, 8, 128)
                                         for k in ("b_out", "ln_g", "ln_b")], axis=1), (3, 0, 1, 2))),
        "ident": c["ident"], "chand": c["chand"], "dft": c["dft"],
    }
    return sh


def kernel(**inputs):
    x = np.asarray(inputs["x"], dtype=np.float32)
    p = np.asarray(inputs["p"], dtype=np.float32)
    B = x.shape[0]
    sh = _prep_shared(inputs)
    if "nc" not in _NC_CACHE:
        _NC_CACHE["nc"] = build(DEPTH)
    nc = _NC_CACHE["nc"]
    in_maps = []
    for b in range(B):
        m = dict(sh)
        m["x"] = np.ascontiguousarray(x[b])
        m["p"] = np.ascontiguousarray(p[:, b])
        in_maps.append(m)
    res = run_bass_kernel_spmd(nc, in_maps, core_ids=list(range(B)))
    return np.stack([np.asarray(r["out"], dtype=np.float32) for r in res.results], axis=0)
```
